# Optimizing a Trainium2 kernel written in Bass

```python
import jax, jax.numpy as jnp
from jax import lax
import numpy as np

D_MODEL = 1024
BATCH = 4
SEQ = 4096
DEPTH = 4
DEC_BATCH = 2
DEC_SEQ = 16384
PAST_LEN = 128

GRID_W = 64
D_CONV = 1024
CONV_W = 3
NA_HEADS = 32
NA_HEAD_DIM = 32
D_ATT = NA_HEADS * NA_HEAD_DIM
NA_KH_MAX = 8
NA_KW = 16
N_BRANCH = 2
D_FF = 2816
RMS_EPS = 1e-6
NEG_INF = -1e30
D_IN = 3 * D_CONV + 3 * D_ATT + N_BRANCH * D_MODEL
SPLITS = (D_CONV, 2 * D_CONV, 3 * D_CONV, 3 * D_CONV + D_ATT, 3 * D_CONV + 2 * D_ATT, 3 * D_CONV + 3 * D_ATT)

kernel_name = 'hybrid_shortconv_natten_macaron_encoder'


def rms_norm(x, g):
    xf = x.astype(jnp.float32)
    y = xf * lax.rsqrt(jnp.mean(xf * xf, axis=-1, keepdims=True) + RMS_EPS)
    return (y * g.astype(jnp.float32)).astype(x.dtype)


def swiglu_ffn(x, w_in, w_out):
    g, u = jnp.split(x @ w_in, 2, axis=-1)
    return (jax.nn.silu(g) * u) @ w_out


def short_conv_mixer(b_gate, c_gate, v, conv_w):
    T = v.shape[1]
    z = c_gate * v
    pad = CONV_W // 2
    zp = jnp.pad(z, ((0, 0), (pad, CONV_W - 1 - pad), (0, 0)))
    conv = zp[:, 0:T] * conv_w[0]
    for j in range(1, CONV_W):
        conv = conv + zp[:, j:j + T] * conv_w[j]
    return b_gate * conv


def neighborhood_attention(q, k, v, rpb):
    bsz, T = q.shape[0], q.shape[1]
    rows = T // GRID_W
    kh = min(NA_KH_MAX, rows)
    q = q.reshape(bsz, rows, GRID_W, NA_HEADS, NA_HEAD_DIM)
    k = k.reshape(bsz, rows, GRID_W, NA_HEADS, NA_HEAD_DIM)
    v = v.reshape(bsz, rows, GRID_W, NA_HEADS, NA_HEAD_DIM)
    qc = jnp.arange(GRID_W)
    kc = jnp.arange(GRID_W)
    c_start = jnp.clip(qc - NA_KW // 2, 0, GRID_W - NA_KW)
    col_ok = (kc[None, :] >= c_start[:, None]) & (kc[None, :] < c_start[:, None] + NA_KW)
    dc = jnp.clip(kc[None, :] - qc[:, None], -(NA_KW - 1), NA_KW - 1) + NA_KW - 1
    mask = jnp.broadcast_to(col_ok[:, None, :], (GRID_W, kh, GRID_W)).reshape(GRID_W, kh * GRID_W)
    scale = NA_HEAD_DIM ** -0.5

    def row_step(r):
        r_start = jnp.clip(r - kh // 2, 0, rows - kh)
        q_r = lax.dynamic_index_in_dim(q, r, axis=1, keepdims=False)
        k_blk = lax.dynamic_slice_in_dim(k, r_start, kh, axis=1).reshape(bsz, kh * GRID_W, NA_HEADS, NA_HEAD_DIM)
        v_blk = lax.dynamic_slice_in_dim(v, r_start, kh, axis=1).reshape(bsz, kh * GRID_W, NA_HEADS, NA_HEAD_DIM)
        dr = r_start + jnp.arange(kh) - r + NA_KH_MAX - 1
        bias = rpb[:, dr[:, None, None], dc[None, :, :]]
        bias = bias.transpose(0, 2, 1, 3).reshape(NA_HEADS, GRID_W, kh * GRID_W)
        s = jnp.einsum('bqhd,bkhd->bhqk', q_r, k_blk).astype(jnp.float32) * scale + bias.astype(jnp.float32)
        s = jnp.where(mask, s, NEG_INF)
        p = jax.nn.softmax(s, axis=-1).astype(v_blk.dtype)
        return jnp.einsum('bhqk,bkhd->bqhd', p, v_blk)

    out = lax.map(row_step, jnp.arange(rows))
    return out.transpose(1, 0, 2, 3, 4).reshape(bsz, T, D_ATT)


def encoder_layer(x, g_ffn1_pre, g_ffn1_post, w_ffn1_in, w_ffn1_out,
                  g_mix_pre, g_mix_post, w_mix_in, b_mix_gate, conv_w, na_rpb,
                  w_conv_branch, w_att_branch, w_mix_out,
                  g_ffn2_pre, g_ffn2_post, w_ffn2_in, w_ffn2_out):
    bsz, T, _ = x.shape
    x = x + 0.5 * rms_norm(swiglu_ffn(rms_norm(x, g_ffn1_pre), w_ffn1_in, w_ffn1_out), g_ffn1_post)
    u = rms_norm(x, g_mix_pre)
    proj = u @ w_mix_in
    cb, cc, cv, q, k, v, gate_logits = jnp.split(proj, SPLITS, axis=-1)
    y_conv = short_conv_mixer(cb, cc, cv, conv_w) @ w_conv_branch
    heads = lambda t: t.reshape(bsz, T, NA_HEADS, NA_HEAD_DIM)
    y_att = neighborhood_attention(heads(q), heads(k), heads(v), na_rpb) @ w_att_branch
    gates = jax.nn.sigmoid((gate_logits + b_mix_gate).astype(jnp.float32)).astype(x.dtype)
    g_conv, g_att = jnp.split(gates, 2, axis=-1)
    mixed = (g_conv * y_conv + g_att * y_att) @ w_mix_out
    x = x + rms_norm(mixed, g_mix_post)
    x = x + 0.5 * rms_norm(swiglu_ffn(rms_norm(x, g_ffn2_pre), w_ffn2_in, w_ffn2_out), g_ffn2_post)
    return x


def setup_inputs(seed: int = 0) -> dict:
    key = jax.random.key(seed)
    ks = jax.random.split(key, 24)
    f32 = jnp.float32

    def w(k, shape, fan_in):
        return jax.random.normal(k, shape, f32) * (fan_in ** -0.5)

    def gain(k):
        return 1.0 + 0.02 * jax.random.normal(k, (DEPTH, D_MODEL), f32)

    return {
        'x_prompt': jax.random.normal(ks[0], (BATCH, SEQ, D_MODEL), f32),
        'x_sample': jax.random.normal(ks[1], (DEC_BATCH, DEC_SEQ, D_MODEL), f32),
        'g_ffn1_pre': gain(ks[2]),
        'g_ffn1_post': gain(ks[3]),
        'w_ffn1_in': w(ks[4], (DEPTH, D_MODEL, 2 * D_FF), D_MODEL),
        'w_ffn1_out': w(ks[5], (DEPTH, D_FF, D_MODEL), D_FF),
        'g_mix_pre': gain(ks[6]),
        'g_mix_post': gain(ks[7]),
        'w_mix_in': w(ks[8], (DEPTH, D_MODEL, D_IN), D_MODEL),
        'b_mix_gate': 0.02 * jax.random.normal(ks[9], (DEPTH, N_BRANCH * D_MODEL), f32),
        'conv_w': w(ks[10], (DEPTH, CONV_W, D_CONV), CONV_W),
        'na_rpb': 0.02 * jax.random.normal(ks[11], (DEPTH, NA_HEADS, 2 * NA_KH_MAX - 1, 2 * NA_KW - 1), f32),
        'w_conv_branch': w(ks[12], (DEPTH, D_CONV, D_MODEL), D_CONV),
        'w_att_branch': w(ks[13], (DEPTH, D_ATT, D_MODEL), D_ATT),
        'w_mix_out': w(ks[14], (DEPTH, D_MODEL, D_MODEL), D_MODEL),
        'g_ffn2_pre': gain(ks[15]),
        'g_ffn2_post': gain(ks[16]),
        'w_ffn2_in': w(ks[17], (DEPTH, D_MODEL, 2 * D_FF), D_MODEL),
        'w_ffn2_out': w(ks[18], (DEPTH, D_FF, D_MODEL), D_FF),
    }


def reference(x_prompt, x_sample, g_ffn1_pre, g_ffn1_post, w_ffn1_in, w_ffn1_out,
              g_mix_pre, g_mix_post, w_mix_in, b_mix_gate, conv_w, na_rpb,
              w_conv_branch, w_att_branch, w_mix_out,
              g_ffn2_pre, g_ffn2_post, w_ffn2_in, w_ffn2_out):
    y_prompt = x_prompt
    y_sample = x_sample
    for l in range(DEPTH):
        layer = lambda h: encoder_layer(
            h, g_ffn1_pre[l], g_ffn1_post[l], w_ffn1_in[l], w_ffn1_out[l],
            g_mix_pre[l], g_mix_post[l], w_mix_in[l], b_mix_gate[l], conv_w[l], na_rpb[l],
            w_conv_branch[l], w_att_branch[l], w_mix_out[l],
            g_ffn2_pre[l], g_ffn2_post[l], w_ffn2_in[l], w_ffn2_out[l])
        y_prompt = layer(y_prompt)
        y_sample = layer(y_sample)
    return (y_prompt, y_sample)
```

```python
import numpy as np
import concourse.bass as bass
import concourse.mybir as mybir
from concourse.bass_utils import run_bass_kernel_spmd

F32 = mybir.dt.float32
BF16 = mybir.dt.bfloat16
AF = mybir.ActivationFunctionType
ALU = mybir.AluOpType

COMPUTE = ("pe", "act", "dve", "pool")
ENGS = ("pe", "act", "dve", "pool", "sp")
KROT = 4
KDMA = 12
NEG = -1.0e30

D = 1024
DFF = 2816
DIN = 8192
GW = 64
NH = 32
RMS_EPS = 1e-6
QSCALE = 32 ** -0.5


class Res:
    __slots__ = ("w", "rs", "excl")

    def __init__(self, excl=False):
        self.w = None
        self.rs = []
        self.excl = excl


class Op:
    __slots__ = ("eng", "fn", "deps", "dma", "xdep", "ms", "idx")

    def __init__(self, eng, fn, dma):
        self.eng = eng
        self.fn = fn
        self.deps = []
        self.dma = dma
        self.xdep = False
        self.ms = None
        self.idx = 0


class Sched:
    def __init__(self, nc):
        self.nc = nc
        self.eng_ops = {e: [] for e in ENGS}
        self.n = 0
        self.dma_count = {}

    def add(self, eng, fn, reads=(), writes=(), dma=None):
        op = Op(eng, fn, dma)
        op.idx = self.n
        self.n += 1
        ex = [r for r in reads if r.excl]
        if ex:
            reads = [r for r in reads if not r.excl]
            writes = list(writes) + ex
        deps = {}
        for r in reads:
            if r.w is not None:
                deps[id(r.w)] = r.w
        for w in writes:
            if w.w is not None:
                deps[id(w.w)] = w.w
            for rd in w.rs:
                deps[id(rd)] = rd
        wid = set(id(w) for w in writes)
        for w in writes:
            w.w = op
            w.rs = []
        for r in reads:
            if id(r) not in wid:
                r.rs.append(op)
        for d in deps.values():
            if d is op:
                continue
            if d.dma is None and d.eng == eng and eng == "pe":
                continue
            op.deps.append(d)
            d.xdep = True
        if dma is not None:
            c = self.dma_count.get(dma, 0)
            op.ms = c
            self.dma_count[dma] = c + 1
        self.eng_ops[eng].append(op)
        return op

    def emit(self):
        nc = self.nc
        for e in COMPUTE:
            m = 0
            for op in self.eng_ops[e]:
                if op.dma is None and op.xdep:
                    op.ms = m
                    m += 1
        sems = {}
        for e in COMPUTE:
            sems[e] = [nc.alloc_semaphore(f"s_{e}{k}") for k in range(KROT)]
        for c in self.dma_count:
            sems["dma_" + c] = [nc.alloc_semaphore(f"d_{c}{k}") for k in range(KDMA)]
        sched = self

        def run_engine(ename):
            def body(eng):
                done_ms = {e: -1 for e in COMPUTE}
                waited = {}
                for op in sched.eng_ops[ename]:
                    for d in sorted(op.deps, key=lambda o: o.idx):
                        if d.dma is None:
                            if done_ms[d.eng] >= d.ms:
                                continue
                            done_ms[d.eng] = d.ms
                            s = sems[d.eng][d.ms % KROT]
                            v = d.ms // KROT + 1
                        else:
                            key = ("dma_" + d.dma, d.ms % KDMA)
                            s = sems[key[0]][key[1]]
                            v = 16 * (d.ms // KDMA + 1)
                            if waited.get(key, 0) >= v:
                                continue
                            waited[key] = v
                        eng.wait_ge(s, v)
                    if op.dma is not None and op.ms >= KDMA:
                        key = ("dma_" + op.dma, op.ms % KDMA)
                        v = 16 * (op.ms // KDMA)
                        if waited.get(key, 0) < v:
                            waited[key] = v
                            eng.wait_ge(sems[key[0]][key[1]], v)
                    last = op.fn(eng)
                    if op.dma is not None:
                        last.then_inc(sems["dma_" + op.dma][op.ms % KDMA], 16)
                    elif op.xdep:
                        last.then_inc(sems[op.eng][op.ms % KROT], 1)
            return body

        with nc.Block() as block:
            block.tensor(run_engine("pe"))
            block.scalar(run_engine("act"))
            block.vector(run_engine("dve"))
            block.gpsimd(run_engine("pool"))
            block.sync(run_engine("sp"))


BLK = 512


class Buf:
    def __init__(self, arena, off, shape, dtype):
        self.arena = arena
        self.off = off
        self.shape = list(shape)
        self.dtype = dtype
        self.esz = 4 if dtype == F32 else 2
        n = int(np.prod(shape))
        self.nbytes = n * self.esz
        ap = arena.t[:, off // 2:(off + self.nbytes) // 2]
        if dtype == F32:
            ap = ap.bitcast(F32)
        if len(shape) == 2:
            ap = ap.rearrange("p (a b) -> p a b", a=shape[0])
        elif len(shape) == 3:
            ap = ap.rearrange("p (a b c) -> p a b c", a=shape[0], b=shape[1])
        elif len(shape) == 4:
            ap = ap.rearrange("p (a b c d) -> p a b c d", a=shape[0], b=shape[1], c=shape[2])
        self.ap = ap
        self.csz = self.nbytes // shape[0]

    def res(self, i=None, n=1):
        if i is None:
            lo, hi = self.off, self.off + self.nbytes
        else:
            lo = self.off + i * self.csz
            hi = lo + n * self.csz
        return self.arena.blocks[lo // BLK:(hi + BLK - 1) // BLK]

    def resb(self, lo, hi):
        lo += self.off
        hi += self.off
        return self.arena.blocks[lo // BLK:(hi + BLK - 1) // BLK]


class Arena:
    def __init__(self, nc, nbytes):
        self.t = nc.alloc_sbuf_tensor("arena", [128, nbytes // 2], BF16)
        self.nbytes = nbytes
        self.blocks = [Res() for _ in range(nbytes // BLK)]
        self.base = 0
        self.cur = 0

    def alloc(self, shape, dtype):
        b = Buf(self, self.cur, shape, dtype)
        self.cur += (b.nbytes + BLK - 1) // BLK * BLK
        assert self.cur <= self.nbytes, f"arena overflow {self.cur} > {self.nbytes}"
        return b

    def fix(self):
        self.base = self.cur

    def reset(self):
        self.cur = self.base


class Cfg:
    def __init__(self, NL=4, RB=96, ES=32, HL=4, seqs=(64, 64, 64, 64, 256, 256), ncores=8):
        self.NL = NL
        self.RB = RB
        self.ES = ES
        self.HL = HL
        self.seqs = list(seqs)
        self.ncores = ncores
        self.H = HL * NL
        self.RT = RB + 2 * self.H
        self.NT = self.RT * GW
        self.edges = [self.H + m * ES for m in range(RB // ES + 1)]
        assert sum(seqs) == RB * ncores
        assert RB % 8 == 0 and HL == 4

    def in_rows(self, l):
        return (self.HL * l, self.RT - self.HL * l)

    def out_rows(self, l):
        return (self.HL * (l + 1), self.RT - self.HL * (l + 1))

    def window(self, rho):
        for E in self.edges:
            if 0 <= rho - E <= 3:
                return (rho - 4, E + 7)
            if 1 <= E - rho <= 3:
                return (E - 8, rho + 3)
        return (rho - 4, rho + 3)

    def slots(self, rho):
        lo, hi = self.window(rho)
        lo -= lo % 2
        return list(range(lo, hi + 1, 2))


def build_program(cfg):
    NL, RT, NT = cfg.NL, cfg.RT, cfg.NT
    nc = bass.Bass("TRN2", target_bir_lowering=False)

    def din(name, shape):
        return nc.dram_tensor(name, list(shape), F32, kind="ExternalInput")

    xin = din("xin", [D, NT])
    w1i = din("w_ffn1_in", [NL, D, 2 * DFF])
    w1o = din("w_ffn1_out", [NL, DFF, D])
    wmi = din("w_mix_in", [NL, D, DIN])
    wcb = din("w_conv_branch", [NL, D, D])
    wab = din("w_att_branch", [NL, D, D])
    wmo = din("w_mix_out", [NL, D, D])
    w2i = din("w_ffn2_in", [NL, D, 2 * DFF])
    w2o = din("w_ffn2_out", [NL, DFF, D])
    gvec_d = din("gvec", [128, NL * 6 * 8])
    bgate_d = din("bgate", [128, NL * 16])
    convw_d = din("convw", [128, NL * 3 * 8])
    ttab_d = din("ttab", [NL, 128, 16 * 2048])
    mtab_d = din("mtab", [128, RT * 6])
    eflag_d = din("eflag", [128, 8])
    ident_d = din("ident", [128, 128])
    onesm_d = din("onesm", [128, 128])
    yout = nc.dram_tensor("yout", [D, cfg.RB * GW], F32, kind="ExternalOutput")

    def dscr(name, shape, dt):
        return nc.dram_tensor(name, list(shape), dt, kind="Internal")

    XR = [dscr("xr0", [D, NT], F32), dscr("xr1", [D, NT], F32)]
    QT = dscr("qt", [D, NT], BF16)
    KT = dscr("kt", [D, NT], BF16)
    VV = dscr("vv", [NT, D], BF16)
    OT = dscr("ot", [D, NT], BF16)
    ZT = dscr("zt", [D, NT], F32)
    CB = dscr("cb", [D, NT], F32)
    GT = dscr("gt", [2 * D, NT], F32)

    def fm(t):
        return t.ap().rearrange("(c p) t -> p c t", p=128)

    dres_tab = {}

    def dres(name, lo, hi):
        tab = dres_tab.setdefault(name, {})
        out = []
        for b in range(lo // 256, (hi + 255) // 256):
            if b not in tab:
                tab[b] = Res()
            out.append(tab[b])
        return out

    S = Sched(nc)
    arena = Arena(nc, 207 * 1024)
    PS = [nc.alloc_psum_tensor(f"ps{i}", [128, 512], F32) for i in range(8)]
    PSR = [Res(excl=True) for _ in range(8)]

    GV = arena.alloc([NL * 6, 8], F32)
    GH = arena.alloc([NL * 6, 8], F32)
    BG = arena.alloc([NL, 16], F32)
    CW = arena.alloc([NL * 3, 8], F32)
    MT = arena.alloc([RT, 6], F32)
    EF = arena.alloc([1, 8], F32)
    EPS = arena.alloc([1, 8], F32)
    IDN = arena.alloc([1, 128], BF16)
    ONM = arena.alloc([1, 128], BF16)
    ON32 = arena.alloc([1, 32], BF16)
    arena.fix()

    def ld(buf, src, eng="sp", cls="c"):
        S.add(eng, lambda e: e.dma_start(out=buf.ap, in_=src), writes=buf.res(), dma=cls)

    ld(GV, gvec_d.ap().rearrange("p (a b) -> p a b", b=8))
    ld(BG, bgate_d.ap().rearrange("p (a b) -> p a b", b=16))
    ld(CW, convw_d.ap().rearrange("p (a b) -> p a b", b=8))
    ld(MT, mtab_d.ap().rearrange("p (a b) -> p a b", b=6))
    ld(EF, eflag_d.ap().rearrange("p (a b) -> p a b", a=1))
    ld(IDN, ident_d.ap().rearrange("p (a b) -> p a b", a=1), eng="pool", cls="w")
    ld(ONM, onesm_d.ap().rearrange("p (a b) -> p a b", a=1), eng="pool", cls="w")
    S.add("dve", lambda e: e.memset(EPS.ap, RMS_EPS), writes=EPS.res())
    S.add("dve", lambda e: e.memset(ON32.ap, 1.0), writes=ON32.res())
    S.add("dve", lambda e: e.tensor_scalar(out=GH.ap, in0=GV.ap, scalar1=0.5, scalar2=None, op0=ALU.mult),
          reads=GV.res(), writes=GH.res())
    ident = IDN.ap[:, 0, :]
    onesm = ONM.ap[:, 0, :]
    ones32 = ON32.ap[:, 0, :]
    eps = EPS.ap[:, 0, 0:1]

    def gv(l, kind, c, half=False):
        return (GH if half else GV).ap[:, l * 6 + kind, c:c + 1]

    def mm_group(bank, terms, reads, extra_writes=()):
        def fn(e):
            last = None
            n = len(terms)
            for i, (lt, rh, out) in enumerate(terms):
                last = e.matmul(out, lhsT=lt, rhs=rh, start=(i == 0), stop=(i == n - 1))
            return last
        S.add("pe", fn, reads=reads, writes=[PSR[bank]] + list(extra_writes))

    def prenorm(X32, XN, SQ, RS, l, kind, stbank=6):
        for c in range(8):
            S.add("act", lambda e, c=c: e.activation(out=SQ.ap[:, c % 2, :], in_=X32.ap[:, c, :], func=AF.Square),
                  reads=X32.res(c), writes=SQ.res(c % 2))
            S.add("pe", lambda e, c=c: e.matmul(PS[stbank][:, :], lhsT=onesm, rhs=SQ.ap[:, c % 2, :], start=(c == 0), stop=(c == 7)),
                  reads=SQ.res(c % 2) + ONM.res(), writes=[PSR[stbank]])
        S.add("act", lambda e: e.activation(out=RS.ap[:, 0, :], in_=PS[stbank][:, :], func=AF.Sqrt, bias=eps, scale=1.0),
              reads=[PSR[stbank]] + EPS.res(), writes=RS.res())
        S.add("dve", lambda e: e.reciprocal(out=RS.ap[:, 0, :], in_=RS.ap[:, 0, :]), reads=RS.res(), writes=RS.res())
        for c in range(8):
            S.add("dve", lambda e, c=c: e.scalar_tensor_tensor(out=XN.ap[:, c, :], in0=X32.ap[:, c, :], scalar=gv(l, kind, c),
                                                               in1=RS.ap[:, 0, :], op0=ALU.mult, op1=ALU.mult),
                  reads=X32.res(c) + RS.res() + GV.res(), writes=XN.res(c))

    def postnorm_residual(X32, Y32, SQ, RS, proj_terms_fn, proj_reads_fn, l, kind, half, ybanks=(4, 5), stbank=6):
        def stat(f):
            S.add("pe", lambda e, f=f: e.matmul(PS[stbank][:, :], lhsT=onesm, rhs=SQ.ap[:, f % 2, :], start=(f == 0), stop=(f == 7)),
                  reads=SQ.res(f % 2) + ONM.res(), writes=[PSR[stbank]])
        for f in range(8):
            b = ybanks[f % 2]
            mm_group(b, proj_terms_fn(f, PS[b][:, :]), proj_reads_fn(f))
            if f > 0:
                stat(f - 1)
            S.add("dve", lambda e, f=f, b=b: e.tensor_copy(out=Y32.ap[:, f, :], in_=PS[b][:, :]), reads=[PSR[b]], writes=Y32.res(f))
            S.add("act", lambda e, f=f: e.activation(out=SQ.ap[:, f % 2, :], in_=Y32.ap[:, f, :], func=AF.Square),
                  reads=Y32.res(f), writes=SQ.res(f % 2))
        stat(7)
        S.add("act", lambda e: e.activation(out=RS.ap[:, 0, :], in_=PS[stbank][:, :], func=AF.Sqrt, bias=eps, scale=1.0),
              reads=[PSR[stbank]] + EPS.res(), writes=RS.res())
        S.add("dve", lambda e: e.reciprocal(out=RS.ap[:, 0, :], in_=RS.ap[:, 0, :]), reads=RS.res(), writes=RS.res())
        for f in range(8):
            S.add("pool", lambda e, f=f: e.tensor_tensor(out=Y32.ap[:, f, :], in0=Y32.ap[:, f, :], in1=RS.ap[:, 0, :], op=ALU.mult),
                  reads=Y32.res(f) + RS.res(), writes=Y32.res(f))
            S.add("dve", lambda e, f=f: e.scalar_tensor_tensor(out=X32.ap[:, f, :], in0=Y32.ap[:, f, :], scalar=gv(l, kind, f, half),
                                                               in1=X32.ap[:, f, :], op0=ALU.mult, op1=ALU.add),
                  reads=Y32.res(f) + X32.res(f) + GH.res() + GV.res(), writes=X32.res(f))

    def ffn_pass(l, w_in, w_out, kpre, kpost, src, src_name, dst, dst_name, rows, dst_off):
        arena.reset()
        WIN = arena.alloc([11, 8, 512], BF16)
        WOUT = arena.alloc([22, 1024], BF16)
        X32 = arena.alloc([8, 512], F32)
        YX = arena.alloc([8, 512], F32)
        XN = Buf(arena, YX.off, [8, 512], BF16)
        A = arena.alloc([22, 512], BF16)
        SQ = arena.alloc([2, 512], BF16)
        RS = arena.alloc([1, 512], F32)
        SG = arena.alloc([2, 512], F32)
        wiv = w_in.ap()[l].rearrange("(k p) n -> p k n", p=128)
        wov = w_out.ap()[l].rearrange("(k p) n -> p k n", p=128)
        for g in range(11):
            S.add("pool", lambda e, g=g: e.dma_start(out=WIN.ap[:, g], in_=wiv[:, :, g * 512:(g + 1) * 512]),
                  writes=WIN.res(g), dma="w")
        for q in range(6):
            k0, k1 = q * 4, min(22, q * 4 + 4)
            S.add("pool", lambda e, k0=k0, k1=k1: e.dma_start(out=WOUT.ap[:, k0:k1, :], in_=wov[:, k0:k1, :]),
                  writes=WOUT.res(k0, k1 - k0), dma="w")
        ra, rb = rows
        for rho0 in range(ra, rb, 8):
            t0 = rho0 * GW
            S.add("sp", lambda e, t0=t0: e.dma_start(out=X32.ap, in_=src[:, :, t0:t0 + 512]),
                  reads=dres(src_name, t0, t0 + 512), writes=X32.res(), dma="x")
            prenorm(X32, XN, SQ, RS, l, kpre)
            for j in range(22):
                for (n, b) in ((j, 2 * (j % 2)), (22 + j, 2 * (j % 2) + 1)):
                    g, i = n // 4, n % 4
                    terms = [(WIN.ap[:, g, k, i * 128:(i + 1) * 128], XN.ap[:, k, :], PS[b][:, :]) for k in range(8)]
                    mm_group(b, terms, WIN.res(g) + XN.res())
                bg, bu = 2 * (j % 2), 2 * (j % 2) + 1
                S.add("act", lambda e, j=j, bg=bg: e.activation(out=SG.ap[:, j % 2, :], in_=PS[bg][:, :], func=AF.Silu),
                      reads=[PSR[bg]], writes=SG.res(j % 2))
                S.add("dve", lambda e, j=j, bu=bu: e.tensor_tensor(out=A.ap[:, j, :], in0=SG.ap[:, j % 2, :], in1=PS[bu][:, :], op=ALU.mult),
                      reads=SG.res(j % 2) + [PSR[bu]], writes=A.res(j))

            def terms_fn(f, out):
                return [(WOUT.ap[:, kk, f * 128:(f + 1) * 128], A.ap[:, kk, :], out) for kk in range(22)]

            def reads_fn(f):
                return WOUT.res() + A.res()
            postnorm_residual(X32, YX, SQ, RS, terms_fn, reads_fn, l, kpost, True)
            d0 = t0 - dst_off
            S.add("sp", lambda e, d0=d0: e.dma_start(out=dst[:, :, d0:d0 + 512], in_=X32.ap),
                  reads=X32.res(), writes=dres(dst_name, d0, d0 + 512), dma="s")

    def mixa_pass(l, src, src_name, rows):
        arena.reset()
        WIN = arena.alloc([16, 8, 512], BF16)
        X32 = arena.alloc([8, 512], F32)
        XN = arena.alloc([8, 512], BF16)
        SQ = arena.alloc([2, 512], BF16)
        RS = arena.alloc([1, 512], F32)
        STG = arena.alloc([4, 512], F32)
        STB = arena.alloc([4, 512], BF16)
        CCT = arena.alloc([2, 512], F32)
        VS = arena.alloc([2, 1024], BF16)
        wiv = wmi.ap()[l].rearrange("(k p) n -> p k n", p=128)
        for g in range(16):
            S.add("pool", lambda e, g=g: e.dma_start(out=WIN.ap[:, g], in_=wiv[:, :, g * 512:(g + 1) * 512]),
                  writes=WIN.res(g), dma="w")
        cbv, ztv, qtv, ktv, gtv = fm(CB), fm(ZT), fm(QT), fm(KT), fm(GT)
        ra, rb = rows
        cnt = {"b": 0, "s": 0, "sb": 0}

        def nb():
            cnt["b"] = (cnt["b"] + 1) % 4
            return cnt["b"]

        for rho0 in range(ra, rb, 8):
            t0 = rho0 * GW
            S.add("sp", lambda e, t0=t0: e.dma_start(out=X32.ap, in_=src[:, :, t0:t0 + 512]),
                  reads=dres(src_name, t0, t0 + 512), writes=X32.res(), dma="x")
            prenorm(X32, XN, SQ, RS, l, 2)

            def proj(n):
                b = nb()
                g, i = n // 4, n % 4
                terms = [(WIN.ap[:, g, k, i * 128:(i + 1) * 128], XN.ap[:, k, :], PS[b][:, :]) for k in range(8)]
                mm_group(b, terms, WIN.res(g) + XN.res())
                return b

            def store(view, name, c, stg, r, t0=t0):
                S.add("sp", lambda e: e.dma_start(out=view[:, c, t0:t0 + 512], in_=stg.ap[:, r, :]),
                      reads=stg.res(r), writes=dres(name, t0, t0 + 512), dma="s")

            for c in range(8):
                b = proj(c)
                r = cnt["s"] = (cnt["s"] + 1) % 4
                S.add("act", lambda e, b=b, r=r: e.activation(out=STG.ap[:, r, :], in_=PS[b][:, :], func=AF.Copy),
                      reads=[PSR[b]], writes=STG.res(r))
                store(cbv, "cb", c, STG, r)
            for c in range(8):
                b = proj(8 + c)
                S.add("dve", lambda e, b=b, c=c: e.tensor_copy(out=CCT.ap[:, c % 2, :], in_=PS[b][:, :]),
                      reads=[PSR[b]], writes=CCT.res(c % 2))
                b = proj(16 + c)
                r = cnt["s"] = (cnt["s"] + 1) % 4
                S.add("dve", lambda e, b=b, c=c, r=r: e.tensor_tensor(out=STG.ap[:, r, :], in0=CCT.ap[:, c % 2, :], in1=PS[b][:, :], op=ALU.mult),
                      reads=[PSR[b]] + CCT.res(c % 2), writes=STG.res(r))
                store(ztv, "zt", c, STG, r)
            for c in range(8):
                b = proj(24 + c)
                r = cnt["sb"] = (cnt["sb"] + 1) % 4
                S.add("act", lambda e, b=b, r=r: e.mul(out=STB.ap[:, r, :], in_=PS[b][:, :], mul=QSCALE),
                      reads=[PSR[b]], writes=STB.res(r))
                store(qtv, "qt", c, STB, r)
            for c in range(8):
                b = proj(32 + c)
                r = cnt["sb"] = (cnt["sb"] + 1) % 4
                S.add("dve", lambda e, b=b, r=r: e.tensor_copy(out=STB.ap[:, r, :], in_=PS[b][:, :]),
                      reads=[PSR[b]], writes=STB.res(r))
                store(ktv, "kt", c, STB, r)
            for tb in range(4):
                for hf in range(2):
                    b = nb()
                    terms = [(XN.ap[:, k, tb * 128:(tb + 1) * 128], WIN.ap[:, 10 + hf, k, :], PS[b][:, :]) for k in range(8)]
                    mm_group(b, terms, WIN.res(10 + hf) + XN.res())
                    if hf == 0:
                        S.add("act", lambda e, b=b, tb=tb: e.activation(out=VS.ap[:, tb % 2, 0:512], in_=PS[b][:, :], func=AF.Copy),
                              reads=[PSR[b]], writes=VS.res(tb % 2))
                    else:
                        S.add("dve", lambda e, b=b, tb=tb: e.tensor_copy(out=VS.ap[:, tb % 2, 512:1024], in_=PS[b][:, :]),
                              reads=[PSR[b]], writes=VS.res(tb % 2))
                tk = t0 + tb * 128
                S.add("sp", lambda e, tk=tk, tb=tb: e.dma_start(out=VV.ap()[tk:tk + 128, :], in_=VS.ap[:, tb % 2, :]),
                      reads=VS.res(tb % 2), writes=dres("vv", tk, tk + 128), dma="s")
            for cg in range(16):
                b = proj(48 + cg)
                r = cnt["s"] = (cnt["s"] + 1) % 4
                S.add("act", lambda e, b=b, r=r, cg=cg: e.activation(out=STG.ap[:, r, :], in_=PS[b][:, :], func=AF.Sigmoid,
                                                                     bias=BG.ap[:, l, cg:cg + 1], scale=1.0),
                      reads=[PSR[b]] + BG.res(), writes=STG.res(r))
                store(gtv, "gt", cg, STG, r)

    def att_pass(l):
        arena.reset()
        TT = arena.alloc([16, 4, 512], BF16)
        KR = arena.alloc([8, 4, 512], BF16)
        VR = arena.alloc([4, 4, 1024], BF16)
        QB = arena.alloc([2, 8, 512], BF16)
        PT = arena.alloc([3, 4, 512], BF16)
        OS = arena.alloc([2, 8, 512], BF16)
        DI = arena.alloc([2, 512], F32)
        ttv = ttab_d.ap()[l].rearrange("p (a b c) -> p a b c", a=16, b=4)
        for a in range(0, 16, 2):
            S.add("pool", lambda e, a=a: e.dma_start(out=TT.ap[:, a:a + 2], in_=ttv[:, a:a + 2]),
                  writes=TT.res(a, 2), dma="w")
        ilo, ihi = cfg.in_rows(l)
        olo, ohi = cfg.out_rows(l)
        rbase = olo - 8
        ktv, qtv, otv = fm(KT), fm(QT), fm(OT)
        ntile = (ohi - olo) // 8

        def load_seg(j):
            r0 = max(rbase + 8 * j, ilo)
            r1 = min(rbase + 8 * j + 8, ihi)
            if r1 <= r0:
                return
            sl = j % 4
            o0 = (r0 - (rbase + 8 * j)) * GW
            n = (r1 - r0) * GW
            t0 = r0 * GW
            S.add("sp", lambda e: e.dma_start(out=KR.ap[:, :, sl, o0:o0 + n], in_=ktv[:, :, t0:t0 + n]),
                  reads=dres("kt", t0, t0 + n), writes=[x for c in range(8) for x in KR.resb(c * KR.csz + sl * 1024, c * KR.csz + sl * 1024 + 1024)], dma="x")
            p0 = o0 // 128
            S.add("sp", lambda e: e.dma_start(out=VR.ap[:, sl, p0:p0 + n // 128, :],
                                              in_=VV.ap()[t0:t0 + n, :].rearrange("(a p) f -> p a f", p=128)),
                  reads=dres("vv", t0, t0 + n), writes=VR.res(sl), dma="x")

        def load_q(i):
            t0 = (olo + 8 * i) * GW
            S.add("sp", lambda e: e.dma_start(out=QB.ap[:, i % 2], in_=qtv[:, :, t0:t0 + 512]),
                  reads=dres("qt", t0, t0 + 512), writes=QB.res(i % 2), dma="x")

        load_seg(0)
        load_seg(1)
        load_seg(2)
        load_q(0)
        state = {"pt": 0, "od": 0}
        for i in range(ntile):
            if i + 1 < ntile:
                load_seg(i + 3)
                load_q(i + 1)
            rho0 = olo + 8 * i
            for qi in range(8):
                rho = rho0 + qi
                slots = cfg.slots(rho)
                ns = len(slots)
                od = state["od"]
                state["od"] = 1 - od
                ob, db = 4 + 2 * od, 5 + 2 * od
                pend = None

                def emit_pv(ps_, pts, s_idx, ns=ns, ob=ob, db=db):
                    sl = ((ps_ - rbase) // 8) % 4
                    pr = ((ps_ - rbase) % 8) // 2

                    def fn(e):
                        last = None
                        for c in range(8):
                            for hp in range(4):
                                e.matmul(PS[ob][32 * hp:32 * hp + 32, c * 64:(c + 1) * 64],
                                         lhsT=VR.ap[:, sl, pr, c * 128 + 32 * hp:c * 128 + 32 * hp + 32],
                                         rhs=PT.ap[:, pts, hp, c * 64:(c + 1) * 64],
                                         start=(s_idx == 0 and c == 0), stop=(s_idx == ns - 1 and c == 7),
                                         skip_group_check=True, tile_position=(0, 32 * hp))
                        for hp in range(4):
                            last = e.matmul(PS[db][32 * hp:32 * hp + 32, :], lhsT=ones32, rhs=PT.ap[:, pts, hp, :],
                                            start=(s_idx == 0), stop=(s_idx == ns - 1), tile_position=(0, 32 * hp))
                        return last
                    S.add("pe", fn, reads=VR.res(sl) + PT.res(pts) + ON32.res(), writes=[PSR[ob], PSR[db]])

                for s_idx, ps_ in enumerate(slots):
                    sl = ((ps_ - rbase) // 8) % 4
                    ko = ((ps_ - rbase) % 8) * GW
                    di = ps_ - rho + 8
                    assert 0 <= di < 16
                    qb = i % 2

                    def fs(e, sl=sl, ko=ko, di=di, qb=qb, qi=qi):
                        last = None
                        for hp in range(4):
                            e.matmul(PS[hp][:, :], lhsT=ident, rhs=TT.ap[:, di, hp, :], start=True, stop=False)
                        for c in range(8):
                            for hp in range(4):
                                last = e.matmul(PS[hp][:, c * 64:(c + 1) * 64],
                                                lhsT=KR.ap[32 * hp:32 * hp + 32, c, sl, ko:ko + 128],
                                                rhs=QB.ap[32 * hp:32 * hp + 32, qb, c, qi * 64:(qi + 1) * 64],
                                                start=False, stop=(c == 7), tile_position=(32 * hp, 0))
                        return last
                    kres = [x for c in range(8) for x in KR.resb(c * KR.csz + sl * 1024, c * KR.csz + sl * 1024 + 1024)]
                    S.add("pe", fs, reads=IDN.res() + TT.res(di) + kres + QB.res(qb), writes=PSR[0:4])
                    pts = state["pt"]
                    state["pt"] = (pts + 1) % 3

                    def fe(e, pts=pts, rho=rho, s_idx=s_idx):
                        last = None
                        for hp in range(4):
                            last = e.activation(out=PT.ap[:, pts, hp, :], in_=PS[hp][:, :], func=AF.Exp,
                                                bias=MT.ap[:, rho, s_idx:s_idx + 1], scale=1.0)
                        return last
                    S.add("act", fe, reads=PSR[0:4] + MT.res(), writes=PT.res(pts))
                    if pend is not None:
                        emit_pv(*pend)
                    pend = (ps_, pts, s_idx)
                emit_pv(*pend)
                osb = i % 2
                S.add("dve", lambda e, od=od, db=db: e.reciprocal(out=DI.ap[:, od, :], in_=PS[db][:, :]),
                      reads=[PSR[db]], writes=DI.res(od))
                S.add("dve", lambda e, od=od, ob=ob, osb=osb, qi=qi: e.tensor_tensor(
                    out=OS.ap[:, osb, :, qi * 64:(qi + 1) * 64],
                    in0=PS[ob][:, :].rearrange("p (c q) -> p c q", c=8),
                    in1=DI.ap[:, od, :].rearrange("p (c q) -> p c q", c=8), op=ALU.mult),
                    reads=[PSR[ob]] + DI.res(od), writes=OS.res(osb))
            t0 = rho0 * GW
            S.add("sp", lambda e, t0=t0, osb=i % 2: e.dma_start(out=otv[:, :, t0:t0 + 512], in_=OS.ap[:, osb]),
                  reads=OS.res(i % 2), writes=dres("ot", t0, t0 + 512), dma="s")

    def mixb_pass(l, src, src_name, dst, dst_name):
        arena.reset()
        WC = arena.alloc([2, 8, 512], BF16)
        WA = arena.alloc([2, 8, 512], BF16)
        WO = arena.alloc([2, 8, 512], BF16)
        X32 = arena.alloc([8, 512], F32)
        ZB = arena.alloc([8, 520], F32)
        CBt = arena.alloc([8, 512], F32)
        OTt = arena.alloc([8, 512], BF16)
        GTt = arena.alloc([16, 512], F32)
        CV = arena.alloc([8, 512], BF16)
        MX = arena.alloc([8, 512], BF16)
        Y32 = arena.alloc([8, 512], F32)
        CA = arena.alloc([2, 512], F32)
        T1 = arena.alloc([2, 512], F32)
        T2 = arena.alloc([2, 512], F32)
        SQ = arena.alloc([2, 512], BF16)
        RS = arena.alloc([1, 512], F32)
        FX = arena.alloc([2, 8], F32)
        for (W, wd) in ((WC, wcb), (WA, wab), (WO, wmo)):
            wv = wd.ap()[l].rearrange("(k p) n -> p k n", p=128)
            for g in range(2):
                S.add("pool", lambda e, W=W, wv=wv, g=g: e.dma_start(out=W.ap[:, g], in_=wv[:, :, g * 512:(g + 1) * 512]),
                      writes=W.res(g), dma="w")
        ztv, cbv, otv, gtv = fm(ZT), fm(CB), fm(OT), fm(GT)
        olo, ohi = cfg.out_rows(l)
        for rho0 in range(olo, ohi, 8):
            t0 = rho0 * GW
            S.add("sp", lambda e, t0=t0: e.dma_start(out=X32.ap, in_=src[:, :, t0:t0 + 512]),
                  reads=dres(src_name, t0, t0 + 512), writes=X32.res(), dma="x")
            S.add("sp", lambda e, t0=t0: e.dma_start(out=ZB.ap[:, :, 0:514], in_=ztv[:, :, t0 - 1:t0 + 513]),
                  reads=dres("zt", t0 - 1, t0 + 513), writes=ZB.res(), dma="x")
            S.add("sp", lambda e, t0=t0: e.dma_start(out=CBt.ap, in_=cbv[:, :, t0:t0 + 512]),
                  reads=dres("cb", t0, t0 + 512), writes=CBt.res(), dma="x")
            S.add("sp", lambda e, t0=t0: e.dma_start(out=OTt.ap, in_=otv[:, :, t0:t0 + 512]),
                  reads=dres("ot", t0, t0 + 512), writes=OTt.res(), dma="x")
            S.add("sp", lambda e, t0=t0: e.dma_start(out=GTt.ap, in_=gtv[:, :, t0:t0 + 512]),
                  reads=dres("gt", t0, t0 + 512), writes=GTt.res(), dma="x")
            fixes = []
            for m, E in enumerate(cfg.edges):
                tE = E * GW
                if 0 <= tE - t0 < 512:
                    fixes.append((tE - t0, 0, tE - t0, m))
                if 0 <= tE - 1 - t0 < 512:
                    fixes.append((tE - 1 - t0, 2, tE - 1 - t0 + 2, m))
            for c in range(8):
                a = c % 2
                S.add("act", lambda e, c=c, a=a: e.mul(out=CA.ap[:, a, :], in_=ZB.ap[:, c, 1:513], mul=CW.ap[:, l * 3 + 1, c:c + 1]),
                      reads=ZB.res(c) + CW.res(), writes=CA.res(a))
                S.add("dve", lambda e, c=c, a=a: e.scalar_tensor_tensor(out=CA.ap[:, a, :], in0=ZB.ap[:, c, 0:512], scalar=CW.ap[:, l * 3 + 0, c:c + 1],
                                                                        in1=CA.ap[:, a, :], op0=ALU.mult, op1=ALU.add),
                      reads=ZB.res(c) + CW.res() + CA.res(a), writes=CA.res(a))
                S.add("dve", lambda e, c=c, a=a: e.scalar_tensor_tensor(out=CA.ap[:, a, :], in0=ZB.ap[:, c, 2:514], scalar=CW.ap[:, l * 3 + 2, c:c + 1],
                                                                        in1=CA.ap[:, a, :], op0=ALU.mult, op1=ALU.add),
                      reads=ZB.res(c) + CW.res() + CA.res(a), writes=CA.res(a))
                for (j, wj, zc, m) in fixes:
                    S.add("dve", lambda e, c=c, a=a, wj=wj, zc=zc: e.tensor_scalar(out=FX.ap[:, a, 0:1], in0=ZB.ap[:, c, zc:zc + 1],
                                                                                  scalar1=CW.ap[:, l * 3 + wj, c:c + 1], scalar2=None, op0=ALU.mult),
                          reads=ZB.res(c) + CW.res(), writes=FX.res(a))
                    S.add("dve", lambda e, a=a, j=j, m=m: e.scalar_tensor_tensor(out=CA.ap[:, a, j:j + 1], in0=FX.ap[:, a, 0:1], scalar=EF.ap[:, 0, m:m + 1],
                                                                                in1=CA.ap[:, a, j:j + 1], op0=ALU.mult, op1=ALU.add),
                          reads=FX.res(a) + EF.res() + CA.res(a), writes=CA.res(a))
                S.add("dve", lambda e, c=c, a=a: e.tensor_tensor(out=CV.ap[:, c, :], in0=CA.ap[:, a, :], in1=CBt.ap[:, c, :], op=ALU.mult),
                      reads=CA.res(a) + CBt.res(c), writes=CV.res(c))
            for f in range(8):
                g, i = f // 4, f % 4
                b1, b2 = 2 * (f % 2), 2 * (f % 2) + 1
                mm_group(b1, [(WC.ap[:, g, k, i * 128:(i + 1) * 128], CV.ap[:, k, :], PS[b1][:, :]) for k in range(8)], WC.res(g) + CV.res())
                mm_group(b2, [(WA.ap[:, g, k, i * 128:(i + 1) * 128], OTt.ap[:, k, :], PS[b2][:, :]) for k in range(8)], WA.res(g) + OTt.res())
                a = f % 2
                S.add("dve", lambda e, f=f, a=a, b1=b1: e.tensor_tensor(out=T1.ap[:, a, :], in0=GTt.ap[:, f, :], in1=PS[b1][:, :], op=ALU.mult),
                      reads=GTt.res(f) + [PSR[b1]], writes=T1.res(a))
                S.add("dve", lambda e, f=f, a=a, b2=b2: e.tensor_tensor(out=T2.ap[:, a, :], in0=GTt.ap[:, 8 + f, :], in1=PS[b2][:, :], op=ALU.mult),
                      reads=GTt.res(8 + f) + [PSR[b2]], writes=T2.res(a))
                S.add("pool", lambda e, f=f, a=a: e.tensor_tensor(out=MX.ap[:, f, :], in0=T1.ap[:, a, :], in1=T2.ap[:, a, :], op=ALU.add),
                      reads=T1.res(a) + T2.res(a), writes=MX.res(f))

            def terms_fn(f, out):
                g, i = f // 4, f % 4
                return [(WO.ap[:, g, k, i * 128:(i + 1) * 128], MX.ap[:, k, :], out) for k in range(8)]

            def reads_fn(f):
                return WO.res(f // 4) + MX.res()
            postnorm_residual(X32, Y32, SQ, RS, terms_fn, reads_fn, l, 3, False)
            S.add("sp", lambda e, t0=t0: e.dma_start(out=dst[:, :, t0:t0 + 512], in_=X32.ap),
                  reads=X32.res(), writes=dres(dst_name, t0, t0 + 512), dma="s")

    import os
    dbg = os.environ.get("K_PASSES", "")
    if dbg:
        names = dbg.split(",")
        l = 0
        if "ffn1" in names:
            ffn_pass(l, w1i, w1o, 0, 1, fm(xin), "xin", fm(XR[0]), "xr0", cfg.in_rows(l), 0)
        if "mixa" in names:
            mixa_pass(l, fm(xin), "xin", cfg.in_rows(l))
        if "att" in names:
            att_pass(l)
        if "mixb" in names:
            mixb_pass(l, fm(xin), "xin", fm(XR[1]), "xr1")
        arena.reset()
        TB = arena.alloc([8, 512], F32)
        dsrc = {"ffn1": XR[0], "mixa": ZT, "att": ZT, "mixb": XR[1]}[names[-1]]
        dn = {"ffn1": "xr0", "mixa": "zt", "att": "zt", "mixb": "xr1"}[names[-1]]
        for rho0 in range(cfg.H, cfg.H + cfg.RB, 8):
            t0 = rho0 * GW
            S.add("sp", lambda e, t0=t0: e.dma_start(out=TB.ap, in_=fm(dsrc)[:, :, t0:t0 + 512]), reads=dres(dn, t0, t0 + 512), writes=TB.res(), dma="x")
            S.add("sp", lambda e, t0=t0: e.dma_start(out=fm(yout)[:, :, t0 - cfg.H * GW:t0 - cfg.H * GW + 512], in_=TB.ap), reads=TB.res(),
                  writes=dres("yout", t0 - cfg.H * GW, t0 - cfg.H * GW + 512), dma="s")
        S.add("sp", lambda e: None, reads=list(dres_tab["yout"].values()))
        S.emit()
        return nc
    for l in range(NL):
        if l == 0:
            src, sname = fm(xin), "xin"
        else:
            src, sname = fm(XR[1]), "xr1"
        ffn_pass(l, w1i, w1o, 0, 1, src, sname, fm(XR[0]), "xr0", cfg.in_rows(l), 0)
        mixa_pass(l, fm(XR[0]), "xr0", cfg.in_rows(l))
        att_pass(l)
        mixb_pass(l, fm(XR[0]), "xr0", fm(XR[1]), "xr1")
        if l == NL - 1:
            ffn_pass(l, w2i, w2o, 4, 5, fm(XR[1]), "xr1", fm(yout), "yout", cfg.out_rows(l), cfg.H * GW)
        else:
            ffn_pass(l, w2i, w2o, 4, 5, fm(XR[1]), "xr1", fm(XR[1]), "xr1", cfg.out_rows(l), 0)
    S.add("sp", lambda e: None, reads=list(dres_tab["yout"].values()))
    S.emit()
    return nc


def host_tables(cfg, c):
    RT, RB, H = cfg.RT, cfg.RB, cfg.H
    starts = np.concatenate([[0], np.cumsum(cfg.seqs)])
    total = int(starts[-1])

    def true_window(g):
        if g < 0 or g >= total:
            return (g - 4, g + 3)
        si = int(np.searchsorted(starts, g, side="right") - 1)
        gs, L = int(starts[si]), cfg.seqs[si]
        rs = int(np.clip(g - gs - 4, 0, L - 8))
        return (gs + rs, gs + rs + 7)

    mt = np.zeros((128, RT, 6), np.float32)
    for rho in range(RT):
        g = RB * c + rho - H
        lo, hi = true_window(g)
        for s, ps in enumerate(cfg.slots(rho)):
            for kr in range(2):
                gk = RB * c + ps + kr - H
                if not (lo <= gk <= hi):
                    mt[kr * 64:(kr + 1) * 64, rho, s] = NEG
    ef = np.zeros((128, 8), np.float32)
    for m, E in enumerate(cfg.edges):
        g = RB * c + E - H
        if g <= 0 or g >= total or g in set(int(x) for x in starts):
            ef[:, m] = -1.0
    return mt.reshape(128, RT * 6), ef


def bias_table(rpb_l):
    kc = np.arange(64)[:, None]
    qc = np.arange(64)[None, :]
    cs = np.clip(qc - 8, 0, 48)
    ok = (kc >= cs) & (kc < cs + 16)
    dci = np.clip(kc - qc, -15, 15) + 15
    B = np.full((NH, 17, 64, 64), NEG, np.float32)
    G = rpb_l[:, :, dci]
    G = np.where(ok[None, None], G, np.float32(NEG))
    B[:, 1:16] = G
    T = np.empty((2, 64, 16, 4, 8, 64), np.float32)
    Bh = B.reshape(8, 4, 17, 64, 64)
    for kr in range(2):
        T[kr] = np.transpose(Bh[:, :, kr:kr + 16], (3, 2, 1, 0, 4))
    return T.reshape(128, 16 * 2048)


def prepare(cfg, inp):
    NL = cfg.NL
    xs = [np.asarray(inp["x_prompt"], np.float32), np.asarray(inp["x_sample"], np.float32)]
    glob = np.concatenate([x.reshape(-1, D) for x in xs], axis=0)
    total_rows = sum(cfg.seqs)
    assert glob.shape[0] == total_rows * GW
    pad = np.zeros((cfg.H * GW, D), np.float32)
    gp = np.concatenate([pad, glob, pad], axis=0)

    def pvec(a, inner):
        a = np.asarray(a, np.float32)
        sh = a.shape[:-1]
        a = a.reshape(*sh, inner, 128)
        a = np.moveaxis(a, -1, 0)
        return np.ascontiguousarray(a.reshape(128, -1))

    gains = np.stack([np.asarray(inp[k], np.float32)[:NL] for k in
                      ("g_ffn1_pre", "g_ffn1_post", "g_mix_pre", "g_mix_post", "g_ffn2_pre", "g_ffn2_post")], axis=1)
    common = {
        "gvec": pvec(gains, 8),
        "bgate": pvec(np.asarray(inp["b_mix_gate"])[:NL], 16),
        "convw": pvec(np.asarray(inp["conv_w"])[:NL], 8),
        "ttab": np.stack([bias_table(np.asarray(inp["na_rpb"], np.float32)[l]) for l in range(NL)], axis=0),
        "ident": np.eye(128, dtype=np.float32),
        "onesm": np.full((128, 128), 1.0 / D, np.float32),
    }
    for k in ("w_ffn1_in", "w_ffn1_out", "w_mix_in", "w_conv_branch", "w_att_branch", "w_mix_out", "w_ffn2_in", "w_ffn2_out"):
        common[k] = np.ascontiguousarray(np.asarray(inp[k], np.float32)[:NL])
    maps = []
    for c in range(cfg.ncores):
        r0 = cfg.RB * c
        xin = np.ascontiguousarray(gp[r0 * GW:(r0 + cfg.RT) * GW].T)
        mt, ef = host_tables(cfg, c)
        m = dict(common)
        m["xin"] = xin
        m["mtab"] = mt
        m["eflag"] = ef
        maps.append(m)
    return maps


def run(cfg, inp):
    nc = build_program(cfg)
    maps = prepare(cfg, inp)
    import os
    ncr = int(os.environ.get("K_NCORES", cfg.ncores))
    res = run_bass_kernel_spmd(nc, maps[:ncr], core_ids=list(range(ncr)))
    ys = [np.asarray(r["yout"], np.float32).T for r in res.results]
    ys = ys + [ys[0]] * (cfg.ncores - ncr)
    glob = np.concatenate(ys, axis=0)
    np_ = int(np.asarray(inp["x_prompt"]).shape[0] * np.asarray(inp["x_prompt"]).shape[1])
    yp = glob[:np_].reshape(np.asarray(inp["x_prompt"]).shape)
    ysm = glob[np_:].reshape(np.asarray(inp["x_sample"]).shape)
    return (np.ascontiguousarray(yp), np.ascontiguousarray(ysm))


def kernel(**inputs):
    cfg = Cfg()
    return run(cfg, inputs)
```

```python
import numpy as np
import concourse.bass as bass
import concourse.mybir as mybir
from concourse.bass_utils import run_bass_kernel_spmd

F32 = mybir.dt.float32
BF16 = mybir.dt.bfloat16
AF = mybir.ActivationFunctionType
ALU = mybir.AluOpType

COMPUTE = ("pe", "act", "dve", "pool")
ENGS = ("pe", "act", "dve", "pool", "sp")
KROT = 4
KDMA = 12
NEG = -1.0e30

D = 1024
DFF = 2816
DIN = 8192
GW = 64
NH = 32
RMS_EPS = 1e-6
QSCALE = 32 ** -0.5


class Res:
    __slots__ = ("w", "rs", "excl")

    def __init__(self, excl=False):
        self.w = None
        self.rs = []
        self.excl = excl


class Op:
    __slots__ = ("eng", "fn", "deps", "dma", "xdep", "ms", "idx")

    def __init__(self, eng, fn, dma):
        self.eng = eng
        self.fn = fn
        self.deps = []
        self.dma = dma
        self.xdep = False
        self.ms = None
        self.idx = 0


class Sched:
    def __init__(self, nc):
        self.nc = nc
        self.eng_ops = {e: [] for e in ENGS}
        self.n = 0
        self.dma_count = {}

    def add(self, eng, fn, reads=(), writes=(), dma=None):
        op = Op(eng, fn, dma)
        op.idx = self.n
        self.n += 1
        ex = [r for r in reads if r.excl]
        if ex:
            reads = [r for r in reads if not r.excl]
            writes = list(writes) + ex
        deps = {}
        for r in reads:
            if r.w is not None:
                deps[id(r.w)] = r.w
        for w in writes:
            if w.w is not None:
                deps[id(w.w)] = w.w
            for rd in w.rs:
                deps[id(rd)] = rd
        wid = set(id(w) for w in writes)
        for w in writes:
            w.w = op
            w.rs = []
        for r in reads:
            if id(r) not in wid:
                r.rs.append(op)
        for d in deps.values():
            if d is op:
                continue
            if d.dma is None and d.eng == eng and eng == "pe":
                continue
            op.deps.append(d)
            d.xdep = True
        if dma is not None:
            c = self.dma_count.get(dma, 0)
            op.ms = c
            self.dma_count[dma] = c + 1
        self.eng_ops[eng].append(op)
        return op

    def emit(self):
        nc = self.nc
        for e in COMPUTE:
            m = 0
            for op in self.eng_ops[e]:
                if op.dma is None and op.xdep:
                    op.ms = m
                    m += 1
        sems = {}
        for e in COMPUTE:
            sems[e] = [nc.alloc_semaphore(f"s_{e}{k}") for k in range(KROT)]
        for c in self.dma_count:
            sems["dma_" + c] = [nc.alloc_semaphore(f"d_{c}{k}") for k in range(KDMA)]
        sched = self

        def run_engine(ename):
            def body(eng):
                done_ms = {e: -1 for e in COMPUTE}
                waited = {}
                for op in sched.eng_ops[ename]:
                    for d in sorted(op.deps, key=lambda o: o.idx):
                        if d.dma is None:
                            if done_ms[d.eng] >= d.ms:
                                continue
                            done_ms[d.eng] = d.ms
                            s = sems[d.eng][d.ms % KROT]
                            v = d.ms // KROT + 1
                        else:
                            key = ("dma_" + d.dma, d.ms % KDMA)
                            s = sems[key[0]][key[1]]
                            v = 16 * (d.ms // KDMA + 1)
                            if waited.get(key, 0) >= v:
                                continue
                            waited[key] = v
                        eng.wait_ge(s, v)
                    if op.dma is not None and op.ms >= KDMA:
                        key = ("dma_" + op.dma, op.ms % KDMA)
                        v = 16 * (op.ms // KDMA)
                        if waited.get(key, 0) < v:
                            waited[key] = v
                            eng.wait_ge(sems[key[0]][key[1]], v)
                    last = op.fn(eng)
                    if op.dma is not None:
                        last.then_inc(sems["dma_" + op.dma][op.ms % KDMA], 16)
                    elif op.xdep:
                        last.then_inc(sems[op.eng][op.ms % KROT], 1)
            return body

        with nc.Block() as block:
            block.tensor(run_engine("pe"))
            block.scalar(run_engine("act"))
            block.vector(run_engine("dve"))
            block.gpsimd(run_engine("pool"))
            block.sync(run_engine("sp"))


BLK = 512


class Buf:
    def __init__(self, arena, off, shape, dtype):
        self.arena = arena
        self.off = off
        self.shape = list(shape)
        self.dtype = dtype
        self.esz = 4 if dtype == F32 else 2
        n = int(np.prod(shape))
        self.nbytes = n * self.esz
        ap = arena.t[:, off // 2:(off + self.nbytes) // 2]
        if dtype == F32:
            ap = ap.bitcast(F32)
        if len(shape) == 2:
            ap = ap.rearrange("p (a b) -> p a b", a=shape[0])
        elif len(shape) == 3:
            ap = ap.rearrange("p (a b c) -> p a b c", a=shape[0], b=shape[1])
        elif len(shape) == 4:
            ap = ap.rearrange("p (a b c d) -> p a b c d", a=shape[0], b=shape[1], c=shape[2])
        self.ap = ap
        self.csz = self.nbytes // shape[0]

    def res(self, i=None, n=1):
        if i is None:
            lo, hi = self.off, self.off + self.nbytes
        else:
            lo = self.off + i * self.csz
            hi = lo + n * self.csz
        return self.arena.blocks[lo // BLK:(hi + BLK - 1) // BLK]

    def resb(self, lo, hi):
        lo += self.off
        hi += self.off
        return self.arena.blocks[lo // BLK:(hi + BLK - 1) // BLK]


class Arena:
    def __init__(self, nc, nbytes):
        self.t = nc.alloc_sbuf_tensor("arena", [128, nbytes // 2], BF16)
        self.nbytes = nbytes
        self.blocks = [Res() for _ in range(nbytes // BLK)]
        self.base = 0
        self.cur = 0

    def alloc(self, shape, dtype):
        b = Buf(self, self.cur, shape, dtype)
        self.cur += (b.nbytes + BLK - 1) // BLK * BLK
        assert self.cur <= self.nbytes, f"arena overflow {self.cur} > {self.nbytes}"
        return b

    def fix(self):
        self.base = self.cur

    def reset(self):
        self.cur = self.base


class Cfg:
    def __init__(self, NL=4, RB=96, ES=32, HL=4, seqs=(64, 64, 64, 64, 256, 256), ncores=8):
        self.NL = NL
        self.RB = RB
        self.ES = ES
        self.HL = HL
        self.seqs = list(seqs)
        self.ncores = ncores
        self.H = HL * NL
        self.RT = RB + 2 * self.H
        self.NT = self.RT * GW
        self.edges = [self.H + m * ES for m in range(RB // ES + 1)]
        assert sum(seqs) == RB * ncores
        assert RB % 8 == 0 and HL == 4

    def in_rows(self, l):
        return (self.HL * l, self.RT - self.HL * l)

    def out_rows(self, l):
        return (self.HL * (l + 1), self.RT - self.HL * (l + 1))

    def window(self, rho):
        for E in self.edges:
            if 0 <= rho - E <= 3:
                return (rho - 4, E + 7)
            if 1 <= E - rho <= 3:
                return (E - 8, rho + 3)
        return (rho - 4, rho + 3)

    def slots(self, rho):
        lo, hi = self.window(rho)
        lo -= lo % 2
        return list(range(lo, hi + 1, 2))


def build_program(cfg):
    NL, RT, NT = cfg.NL, cfg.RT, cfg.NT
    nc = bass.Bass("TRN2", target_bir_lowering=False)

    def din(name, shape):
        return nc.dram_tensor(name, list(shape), F32, kind="ExternalInput")

    xin = din("xin", [D, NT])
    w1i = din("w_ffn1_in", [NL, D, 2 * DFF])
    w1o = din("w_ffn1_out", [NL, DFF, D])
    wmi = din("w_mix_in", [NL, D, DIN])
    wcb = din("w_conv_branch", [NL, D, D])
    wab = din("w_att_branch", [NL, D, D])
    wmo = din("w_mix_out", [NL, D, D])
    w2i = din("w_ffn2_in", [NL, D, 2 * DFF])
    w2o = din("w_ffn2_out", [NL, DFF, D])
    gvec_d = din("gvec", [128, NL * 6 * 8])
    bgate_d = din("bgate", [128, NL * 16])
    convw_d = din("convw", [128, NL * 3 * 8])
    ttab_d = din("ttab", [NL, 128, 16 * 2048])
    mtab_d = din("mtab", [128, RT * 6])
    eflag_d = din("eflag", [128, 8])
    ident_d = din("ident", [128, 128])
    onesm_d = din("onesm", [128, 128])
    yout = nc.dram_tensor("yout", [D, cfg.RB * GW], F32, kind="ExternalOutput")

    def dscr(name, shape, dt):
        return nc.dram_tensor(name, list(shape), dt, kind="Internal")

    XR = [dscr("xr0", [D, NT], F32), dscr("xr1", [D, NT], F32)]
    QT = dscr("qt", [D, NT], BF16)
    KT = dscr("kt", [D, NT], BF16)
    VV = dscr("vv", [NT, D], BF16)
    OT = dscr("ot", [D, NT], BF16)
    ZT = dscr("zt", [D, NT], F32)
    CB = dscr("cb", [D, NT], F32)
    GT = dscr("gt", [2 * D, NT], F32)

    def fm(t):
        return t.ap().rearrange("(c p) t -> p c t", p=128)

    dres_tab = {}

    NCH = {"gt": 16, "vv": 1}

    def dres(name, lo, hi, c=None):
        tab = dres_tab.setdefault(name, {})
        n = NCH.get(name, 8)
        cs = range(n) if c is None else ([c] if isinstance(c, int) else c)
        out = []
        for cc in cs:
            for b in range(lo // 256, (hi + 255) // 256):
                key = (cc, b)
                if key not in tab:
                    tab[key] = Res()
                out.append(tab[key])
        return out

    S = Sched(nc)
    arena = Arena(nc, 207 * 1024)
    PSA = nc.alloc_psum_tensor("psa", [128, 4096], F32)
    PS = [PSA[:, 512 * i:512 * (i + 1)] for i in range(8)]
    PSR = [Res(excl=True) for _ in range(8)]

    GV = arena.alloc([NL * 6, 8], F32)
    GH = arena.alloc([NL * 6, 8], F32)
    BG = arena.alloc([NL, 16], F32)
    CW = arena.alloc([NL * 3, 8], F32)
    MT = arena.alloc([RT, 6], F32)
    EF = arena.alloc([1, 8], F32)
    EPS = arena.alloc([1, 8], F32)
    IDN = arena.alloc([1, 128], BF16)
    ONM = arena.alloc([1, 128], BF16)
    ON32 = arena.alloc([1, 32], BF16)
    arena.fix()

    def ld(buf, src, eng="sp", cls="c"):
        S.add(eng, lambda e: e.dma_start(out=buf.ap, in_=src), writes=buf.res(), dma=cls)

    ld(GV, gvec_d.ap().rearrange("p (a b) -> p a b", b=8))
    ld(BG, bgate_d.ap().rearrange("p (a b) -> p a b", b=16))
    ld(CW, convw_d.ap().rearrange("p (a b) -> p a b", b=8))
    ld(MT, mtab_d.ap().rearrange("p (a b) -> p a b", b=6))
    ld(EF, eflag_d.ap().rearrange("p (a b) -> p a b", a=1))
    ld(IDN, ident_d.ap().rearrange("p (a b) -> p a b", a=1), eng="pool", cls="w")
    ld(ONM, onesm_d.ap().rearrange("p (a b) -> p a b", a=1), eng="pool", cls="w")
    S.add("dve", lambda e: e.memset(EPS.ap, RMS_EPS), writes=EPS.res())
    S.add("dve", lambda e: e.memset(ON32.ap, 1.0), writes=ON32.res())
    S.add("dve", lambda e: e.tensor_scalar(out=GH.ap, in0=GV.ap, scalar1=0.5, scalar2=None, op0=ALU.mult),
          reads=GV.res(), writes=GH.res())
    ident = IDN.ap[:, 0, :]
    onesm = ONM.ap[:, 0, :]
    ones32 = ON32.ap[:, 0, :]
    eps = EPS.ap[:, 0, 0:1]

    def gv(l, kind, c, half=False):
        return (GH if half else GV).ap[:, l * 6 + kind, c:c + 1]

    def mm_group(bank, terms, reads, extra_writes=()):
        def fn(e):
            last = None
            n = len(terms)
            for i, (lt, rh, out) in enumerate(terms):
                last = e.matmul(out, lhsT=lt, rhs=rh, start=(i == 0), stop=(i == n - 1))
            return last
        S.add("pe", fn, reads=reads, writes=[PSR[bank]] + list(extra_writes))

    def pn_square(X32, SQ, c):
        S.add("act", lambda e: e.activation(out=SQ.ap[:, c % 2, :], in_=X32.ap[:, c, :], func=AF.Square),
              reads=X32.res(c), writes=SQ.res(c % 2))

    def pn_stat(SQ, c, stbank):
        S.add("pe", lambda e: e.matmul(PS[stbank][:, :], lhsT=onesm, rhs=SQ.ap[:, c % 2, :], start=(c == 0), stop=(c == 7)),
              reads=SQ.res(c % 2) + ONM.res(), writes=[PSR[stbank]])

    def pn_rstd(RS, r, stbank):
        S.add("act", lambda e: e.activation(out=RS.ap[:, r, :], in_=PS[stbank][:, :], func=AF.Sqrt, bias=eps, scale=1.0),
              reads=[PSR[stbank]] + EPS.res(), writes=RS.res(r))
        S.add("dve", lambda e: e.reciprocal(out=RS.ap[:, r, :], in_=RS.ap[:, r, :]), reads=RS.res(r), writes=RS.res(r))

    def pn_apply(X32, XN, RS, r, l, kind):
        for c in range(8):
            S.add("dve", lambda e, c=c: e.scalar_tensor_tensor(out=XN.ap[:, c, :], in0=X32.ap[:, c, :], scalar=gv(l, kind, c),
                                                               in1=RS.ap[:, r, :], op0=ALU.mult, op1=ALU.mult),
                  reads=X32.res(c) + RS.res(r) + GV.res(), writes=XN.res(c))

    def prenorm(X32, XN, SQ, RS, l, kind, stbank=6):
        for c in range(8):
            pn_square(X32, SQ, c)
            pn_stat(SQ, c, stbank)
        pn_rstd(RS, 0, stbank)
        pn_apply(X32, XN, RS, 0, l, kind)

    def post_part1(Y32, SQ, RS, r, proj_terms_fn, proj_reads_fn, ybanks=(4, 5), stbank=6):
        for f in range(8):
            b = ybanks[f % 2]
            mm_group(b, proj_terms_fn(f, PS[b][:, :]), proj_reads_fn(f))
            if f > 0:
                pn_stat(SQ, f - 1, stbank)
            S.add("act", lambda e, f=f, b=b: e.activation(out=Y32.ap[:, f, :], in_=PS[b][:, :], func=AF.Copy), reads=[PSR[b]], writes=Y32.res(f))
            pn_square(Y32, SQ, f)
        pn_stat(SQ, 7, stbank)
        pn_rstd(RS, r, stbank)

    def post_part2(XR, Y32, RS, r, l, kind, half, f, after=None):
        S.add("pool", lambda e: e.tensor_tensor(out=Y32.ap[:, f, :], in0=Y32.ap[:, f, :], in1=RS.ap[:, r, :], op=ALU.mult),
              reads=Y32.res(f) + RS.res(r), writes=Y32.res(f))
        S.add("dve", lambda e: e.scalar_tensor_tensor(out=XR.ap[:, f, :], in0=Y32.ap[:, f, :], scalar=gv(l, kind, f, half),
                                                      in1=XR.ap[:, f, :], op0=ALU.mult, op1=ALU.add),
              reads=Y32.res(f) + XR.res(f) + GH.res() + GV.res(), writes=XR.res(f))
        if after is not None:
            after(f)

    def load_x(X32, src, src_name, t0):
        for c in range(8):
            S.add("sp", lambda e, c=c: e.dma_start(out=X32.ap[:, c, :], in_=src[:, c, t0:t0 + 512]),
                  reads=dres(src_name, t0, t0 + 512, c), writes=X32.res(c), dma="x")

    def ffn_pass(l, w_in, w_out, kpre, kpost, src, src_name, dst, dst_name, rows, dst_off):
        arena.reset()
        WIN = arena.alloc([11, 8, 512], BF16)
        WOUT = arena.alloc([22, 1024], BF16)
        X32 = arena.alloc([8, 512], F32)
        YX = arena.alloc([8, 512], F32)
        XN = Buf(arena, YX.off, [8, 512], BF16)
        A = arena.alloc([22, 512], BF16)
        XRA = Buf(arena, A.off, [8, 512], F32)
        SQ = arena.alloc([2, 512], BF16)
        RS = arena.alloc([3, 512], F32)
        SG = arena.alloc([2, 512], F32)
        wiv = w_in.ap()[l].rearrange("(k p) n -> p k n", p=128)
        wov = w_out.ap()[l].rearrange("(k p) n -> p k n", p=128)
        for g in range(11):
            S.add("pool", lambda e, g=g: e.dma_start(out=WIN.ap[:, g], in_=wiv[:, :, g * 512:(g + 1) * 512]),
                  writes=WIN.res(g), dma="w")
        for q in range(6):
            k0, k1 = q * 4, min(22, q * 4 + 4)
            S.add("pool", lambda e, k0=k0, k1=k1: e.dma_start(out=WOUT.ap[:, k0:k1, :], in_=wov[:, k0:k1, :]),
                  writes=WOUT.res(k0, k1 - k0), dma="w")
        ra, rb = rows
        tiles = list(range(ra, rb, 8))
        load_x(X32, src, src_name, tiles[0] * GW)
        for c in range(8):
            pn_square(X32, SQ, c)
            pn_stat(SQ, c, 7)
        pn_rstd(RS, 0, 7)
        pn_apply(X32, XN, RS, 0, l, kpre)
        for ti, rho0 in enumerate(tiles):
            t0 = rho0 * GW
            nxt = tiles[ti + 1] * GW if ti + 1 < len(tiles) else None
            rn = (ti + 1) % 2
            for j in range(22):
                for (n, b) in ((j, 2 * (j % 2)), (22 + j, 2 * (j % 2) + 1)):
                    g, i = n // 4, n % 4
                    terms = [(WIN.ap[:, g, k, i * 128:(i + 1) * 128], XN.ap[:, k, :], PS[b][:, :]) for k in range(8)]
                    mm_group(b, terms, WIN.res(g) + XN.res())
                bg, bu = 2 * (j % 2), 2 * (j % 2) + 1
                S.add("act", lambda e, j=j, bg=bg: e.activation(out=SG.ap[:, j % 2, :], in_=PS[bg][:, :], func=AF.Silu),
                      reads=[PSR[bg]], writes=SG.res(j % 2))
                S.add("dve", lambda e, j=j, bu=bu: e.tensor_tensor(out=A.ap[:, j, :], in0=SG.ap[:, j % 2, :], in1=PS[bu][:, :], op=ALU.mult),
                      reads=SG.res(j % 2) + [PSR[bu]], writes=A.res(j))
                if nxt is not None:
                    if j == 3:
                        load_x(X32, src, src_name, nxt)
                    if 10 <= j < 18:
                        pn_square(X32, SQ, j - 10)
                    if 11 <= j < 19:
                        pn_stat(SQ, j - 11, 7)
                    if j == 19:
                        pn_rstd(RS, rn, 7)

            def terms_fn(f, out):
                return [(WOUT.ap[:, kk, f * 128:(f + 1) * 128], A.ap[:, kk, :], out) for kk in range(22)]

            def reads_fn(f):
                return WOUT.res() + A.res()
            post_part1(YX, SQ, RS, 2, terms_fn, reads_fn)
            for f in range(8):
                S.add("sp", lambda e, f=f, t0=t0: e.dma_start(out=XRA.ap[:, f, :], in_=src[:, f, t0:t0 + 512]),
                      reads=dres(src_name, t0, t0 + 512, f), writes=XRA.res(f), dma="x")
            d0 = t0 - dst_off

            def st(f, d0=d0):
                S.add("pool", lambda e: e.dma_start(out=dst[:, f, d0:d0 + 512], in_=XRA.ap[:, f, :]),
                      reads=XRA.res(f), writes=dres(dst_name, d0, d0 + 512, f), dma="s")
            for f in range(8):
                post_part2(XRA, YX, RS, 2, l, kpost, True, f, after=st)
                if f == 3 and nxt is not None:
                    pn_apply(X32, XN, RS, rn, l, kpre)

    def mixa_pass(l, src, src_name, rows):
        arena.reset()
        WIN = arena.alloc([16, 8, 512], BF16)
        X32 = arena.alloc([8, 512], F32)
        XN = arena.alloc([8, 512], BF16)
        SQ = arena.alloc([2, 512], BF16)
        RS = arena.alloc([1, 512], F32)
        STG = arena.alloc([4, 512], F32)
        STB = arena.alloc([4, 512], BF16)
        CCT = arena.alloc([2, 512], F32)
        VS = arena.alloc([2, 1024], BF16)
        wiv = wmi.ap()[l].rearrange("(k p) n -> p k n", p=128)
        for g in range(16):
            S.add("pool", lambda e, g=g: e.dma_start(out=WIN.ap[:, g], in_=wiv[:, :, g * 512:(g + 1) * 512]),
                  writes=WIN.res(g), dma="w")
        cbv, ztv, qtv, ktv, gtv = fm(CB), fm(ZT), fm(QT), fm(KT), fm(GT)
        ra, rb = rows
        cnt = {"b": 0, "s": 0, "sb": 0}

        def nb():
            cnt["b"] = (cnt["b"] + 1) % 4
            return cnt["b"]

        for rho0 in range(ra, rb, 8):
            t0 = rho0 * GW
            load_x(X32, src, src_name, t0)
            prenorm(X32, XN, SQ, RS, l, 2)

            def proj(n):
                b = nb()
                g, i = n // 4, n % 4
                terms = [(WIN.ap[:, g, k, i * 128:(i + 1) * 128], XN.ap[:, k, :], PS[b][:, :]) for k in range(8)]
                mm_group(b, terms, WIN.res(g) + XN.res())
                return b

            def store(view, name, c, stg, r, t0=t0):
                S.add("pool", lambda e: e.dma_start(out=view[:, c, t0:t0 + 512], in_=stg.ap[:, r, :]),
                      reads=stg.res(r), writes=dres(name, t0, t0 + 512, c), dma="s")

            for c in range(8):
                b = proj(c)
                r = cnt["s"] = (cnt["s"] + 1) % 4
                S.add("act", lambda e, b=b, r=r: e.activation(out=STG.ap[:, r, :], in_=PS[b][:, :], func=AF.Copy),
                      reads=[PSR[b]], writes=STG.res(r))
                store(cbv, "cb", c, STG, r)
            for c in range(8):
                b = proj(8 + c)
                S.add("dve", lambda e, b=b, c=c: e.tensor_copy(out=CCT.ap[:, c % 2, :], in_=PS[b][:, :]),
                      reads=[PSR[b]], writes=CCT.res(c % 2))
                b = proj(16 + c)
                r = cnt["s"] = (cnt["s"] + 1) % 4
                S.add("dve", lambda e, b=b, c=c, r=r: e.tensor_tensor(out=STG.ap[:, r, :], in0=CCT.ap[:, c % 2, :], in1=PS[b][:, :], op=ALU.mult),
                      reads=[PSR[b]] + CCT.res(c % 2), writes=STG.res(r))
                store(ztv, "zt", c, STG, r)
            for c in range(8):
                b = proj(24 + c)
                r = cnt["sb"] = (cnt["sb"] + 1) % 4
                S.add("act", lambda e, b=b, r=r: e.mul(out=STB.ap[:, r, :], in_=PS[b][:, :], mul=QSCALE),
                      reads=[PSR[b]], writes=STB.res(r))
                store(qtv, "qt", c, STB, r)
            for c in range(8):
                b = proj(32 + c)
                r = cnt["sb"] = (cnt["sb"] + 1) % 4
                S.add("dve", lambda e, b=b, r=r: e.tensor_copy(out=STB.ap[:, r, :], in_=PS[b][:, :]),
                      reads=[PSR[b]], writes=STB.res(r))
                store(ktv, "kt", c, STB, r)
            for tb in range(4):
                for hf in range(2):
                    b = nb()
                    terms = [(XN.ap[:, k, tb * 128:(tb + 1) * 128], WIN.ap[:, 10 + hf, k, :], PS[b][:, :]) for k in range(8)]
                    mm_group(b, terms, WIN.res(10 + hf) + XN.res())
                    if hf == 0:
                        S.add("act", lambda e, b=b, tb=tb: e.activation(out=VS.ap[:, tb % 2, 0:512], in_=PS[b][:, :], func=AF.Copy),
                              reads=[PSR[b]], writes=VS.res(tb % 2))
                    else:
                        S.add("dve", lambda e, b=b, tb=tb: e.tensor_copy(out=VS.ap[:, tb % 2, 512:1024], in_=PS[b][:, :]),
                              reads=[PSR[b]], writes=VS.res(tb % 2))
                tk = t0 + tb * 128
                S.add("pool", lambda e, tk=tk, tb=tb: e.dma_start(out=VV.ap()[tk:tk + 128, :], in_=VS.ap[:, tb % 2, :]),
                      reads=VS.res(tb % 2), writes=dres("vv", tk, tk + 128), dma="s")
            for cg in range(16):
                b = proj(48 + cg)
                r = cnt["s"] = (cnt["s"] + 1) % 4
                S.add("act", lambda e, b=b, r=r, cg=cg: e.activation(out=STG.ap[:, r, :], in_=PS[b][:, :], func=AF.Sigmoid,
                                                                     bias=BG.ap[:, l, cg:cg + 1], scale=1.0),
                      reads=[PSR[b]] + BG.res(), writes=STG.res(r))
                store(gtv, "gt", cg, STG, r)

    def att_pass(l):
        arena.reset()
        TT = arena.alloc([16, 4, 512], BF16)
        KR = arena.alloc([8, 4, 512], BF16)
        VR = arena.alloc([4, 4, 1024], BF16)
        QB = arena.alloc([2, 8, 512], BF16)
        PT = arena.alloc([4, 4, 512], BF16)
        PTF = Buf(arena, PT.off, [4, 2048], BF16)
        TTF = Buf(arena, TT.off, [16, 2048], BF16)
        OS = arena.alloc([2, 8, 512], BF16)
        DI = arena.alloc([2, 512], F32)
        ttv = ttab_d.ap()[l].rearrange("p (a b c) -> p a b c", a=16, b=4)
        for a_ in range(0, 16, 2):
            S.add("pool", lambda e, a_=a_: e.dma_start(out=TT.ap[:, a_:a_ + 2], in_=ttv[:, a_:a_ + 2]),
                  writes=TT.res(a_, 2), dma="w")
        for di in range(16):
            S.add("act", lambda e, di=di: e.activation(out=TT.ap[:, di], in_=TT.ap[:, di], func=AF.Exp),
                  reads=TT.res(di), writes=TT.res(di))
        ilo, ihi = cfg.in_rows(l)
        olo, ohi = cfg.out_rows(l)
        rbase = olo - 8
        ktv, qtv, otv = fm(KT), fm(QT), fm(OT)
        ntile = (ohi - olo) // 8

        def kres(sl):
            return [x for c in range(8) for x in KR.resb(c * KR.csz + sl * 1024, c * KR.csz + sl * 1024 + 1024)]

        def load_seg(j):
            r0 = max(rbase + 8 * j, ilo)
            r1 = min(rbase + 8 * j + 8, ihi)
            if r1 <= r0:
                return
            sl = j % 4
            o0 = (r0 - (rbase + 8 * j)) * GW
            n = (r1 - r0) * GW
            t0 = r0 * GW
            S.add("sp", lambda e: e.dma_start(out=KR.ap[:, :, sl, o0:o0 + n], in_=ktv[:, :, t0:t0 + n]),
                  reads=dres("kt", t0, t0 + n), writes=kres(sl), dma="x")
            p0 = o0 // 128
            S.add("sp", lambda e: e.dma_start(out=VR.ap[:, sl, p0:p0 + n // 128, :],
                                              in_=VV.ap()[t0:t0 + n, :].rearrange("(a p) f -> p a f", p=128)),
                  reads=dres("vv", t0, t0 + n), writes=VR.res(sl), dma="x")

        def load_q(i):
            t0 = (olo + 8 * i) * GW
            S.add("sp", lambda e: e.dma_start(out=QB.ap[:, i % 2], in_=qtv[:, :, t0:t0 + 512]),
                  reads=dres("qt", t0, t0 + 512), writes=QB.res(i % 2), dma="x")

        pend = []

        def flush(keep=0):
            while len(pend) > keep:
                for f_ in pend.pop(0):
                    f_()

        def emit_pv(sl, pr, pts, first, last, ob, db):
            def fn(e):
                lastm = None
                for c in range(8):
                    for hp in range(4):
                        e.matmul(PS[ob][32 * hp:32 * hp + 32, c * 64:(c + 1) * 64],
                                 lhsT=VR.ap[:, sl, pr, c * 128 + 32 * hp:c * 128 + 32 * hp + 32],
                                 rhs=PT.ap[:, pts, hp, c * 64:(c + 1) * 64],
                                 start=(first and c == 0), stop=(last and c == 7),
                                 skip_group_check=True, tile_position=(0, 32 * hp))
                for hp in range(4):
                    lastm = e.matmul(PS[db][32 * hp:32 * hp + 32, :], lhsT=ones32, rhs=PT.ap[:, pts, hp, :],
                                     start=first, stop=last, tile_position=(0, 32 * hp))
                return lastm
            S.add("pe", fn, reads=VR.res(sl) + PT.res(pts) + ON32.res(), writes=[PSR[ob], PSR[db]])

        def finalize(od, ob, db, osb, qi):
            S.add("dve", lambda e: e.reciprocal(out=DI.ap[:, od, :], in_=PS[db][:, :]), reads=[PSR[db]], writes=DI.res(od))
            S.add("dve", lambda e: e.tensor_tensor(
                out=OS.ap[:, osb, :, qi * 64:(qi + 1) * 64],
                in0=PS[ob][:, :].rearrange("p (c q) -> p c q", c=8),
                in1=DI.ap[:, od, :].rearrange("p (c q) -> p c q", c=8), op=ALU.mult),
                reads=[PSR[ob]] + DI.res(od), writes=OS.res(osb))

        def store_o(i):
            t0 = (olo + 8 * i) * GW
            S.add("sp", lambda e: e.dma_start(out=otv[:, :, t0:t0 + 512], in_=OS.ap[:, i % 2]),
                  reads=OS.res(i % 2), writes=dres("ot", t0, t0 + 512), dma="so")

        load_seg(0)
        load_seg(1)
        load_seg(2)
        load_q(0)
        state = {"pt": 0, "od": 0}
        for i in range(ntile):
            flush()
            if i + 1 < ntile:
                load_seg(i + 3)
                load_q(i + 1)
            rho0 = olo + 8 * i
            qb = i % 2
            for qi in range(8):
                rho = rho0 + qi
                slots = cfg.slots(rho)
                ns = len(slots)
                od = state["od"]
                state["od"] = 1 - od
                ob, db = 4 + 2 * od, 5 + 2 * od
                for s_idx, ps_ in enumerate(slots):
                    sl = ((ps_ - rbase) // 8) % 4
                    ko = ((ps_ - rbase) % 8) * GW
                    pr = ((ps_ - rbase) % 8) // 2
                    di = ps_ - rho + 8
                    assert 0 <= di < 16
                    pts = state["pt"]
                    state["pt"] = (pts + 1) % 4
                    for H in range(2):
                        hps = (2 * H, 2 * H + 1)

                        def fs(e, sl=sl, ko=ko, qi=qi, hps=hps, qb=qb):
                            lastm = None
                            for c in range(8):
                                for hp in hps:
                                    lastm = e.matmul(PS[hp][:, c * 64:(c + 1) * 64],
                                                     lhsT=KR.ap[32 * hp:32 * hp + 32, c, sl, ko:ko + 128],
                                                     rhs=QB.ap[32 * hp:32 * hp + 32, qb, c, qi * 64:(qi + 1) * 64],
                                                     start=(c == 0), stop=(c == 7), skip_group_check=True,
                                                     tile_position=(32 * hp, 0))
                            return lastm
                        S.add("pe", fs, reads=kres(sl) + QB.res(qb), writes=[PSR[hps[0]], PSR[hps[1]]])

                        pres = PT.resb(pts * PT.csz + hps[0] * 1024, pts * PT.csz + hps[1] * 1024 + 1024)
                        S.add("act", lambda e, pts=pts, rho=rho, s_idx=s_idx, H=H: e.activation(
                            out=PTF.ap[:, pts, 1024 * H:1024 * H + 1024], in_=PSA[:, 1024 * H:1024 * H + 1024], func=AF.Exp,
                            bias=MT.ap[:, rho, s_idx:s_idx + 1], scale=1.0),
                            reads=[PSR[hps[0]], PSR[hps[1]]] + MT.res(), writes=pres)
                        if H == 0:
                            for hp in hps:
                                eng = "pool" if hp == 0 else "dve"
                                pr1 = PT.resb(pts * PT.csz + hp * 1024, pts * PT.csz + hp * 1024 + 1024)
                                S.add(eng, lambda e, pts=pts, hp=hp, di=di: e.tensor_tensor(
                                    out=PT.ap[:, pts, hp, :], in0=PT.ap[:, pts, hp, :], in1=TT.ap[:, di, hp, :], op=ALU.mult),
                                    reads=TT.res(di) + pr1, writes=pr1)
                        else:
                            S.add("dve", lambda e, pts=pts, di=di: e.tensor_tensor(
                                out=PTF.ap[:, pts, 1024:2048], in0=PTF.ap[:, pts, 1024:2048], in1=TTF.ap[:, di, 1024:2048], op=ALU.mult),
                                reads=TT.res(di) + pres, writes=pres)
                    flush(keep=1)
                    bundle = [lambda sl=sl, pr=pr, pts=pts, first=(s_idx == 0), last=(s_idx == ns - 1), ob=ob, db=db:
                              emit_pv(sl, pr, pts, first, last, ob, db)]
                    if s_idx == ns - 1:
                        bundle.append(lambda od=od, ob=ob, db=db, osb=i % 2, qi=qi: finalize(od, ob, db, osb, qi))
                        if qi == 7:
                            bundle.append(lambda i=i: store_o(i))
                    pend.append(bundle)
        flush()

    def mixb_pass(l, src, src_name, dst, dst_name):
        arena.reset()
        WC = arena.alloc([2, 8, 512], BF16)
        WA = arena.alloc([2, 8, 512], BF16)
        WO = arena.alloc([2, 8, 512], BF16)
        X32 = arena.alloc([8, 512], F32)
        ZB = arena.alloc([8, 520], F32)
        CBt = arena.alloc([8, 512], F32)
        OTt = arena.alloc([8, 512], BF16)
        GTt = arena.alloc([16, 512], F32)
        CV = arena.alloc([2, 8, 512], BF16)
        MX = arena.alloc([8, 512], BF16)
        Y32 = arena.alloc([8, 512], F32)
        CA = arena.alloc([2, 512], F32)
        T1 = arena.alloc([2, 512], F32)
        T2 = arena.alloc([2, 512], F32)
        SQ = arena.alloc([2, 512], BF16)
        RS = arena.alloc([1, 512], F32)
        FX = arena.alloc([2, 8], F32)
        for (W, wd) in ((WC, wcb), (WA, wab), (WO, wmo)):
            wv = wd.ap()[l].rearrange("(k p) n -> p k n", p=128)
            for g in range(2):
                S.add("pool", lambda e, W=W, wv=wv, g=g: e.dma_start(out=W.ap[:, g], in_=wv[:, :, g * 512:(g + 1) * 512]),
                      writes=W.res(g), dma="w")
        ztv, cbv, otv, gtv = fm(ZT), fm(CB), fm(OT), fm(GT)
        olo, ohi = cfg.out_rows(l)
        tiles = list(range(olo, ohi, 8))

        def cvres(par, c=None):
            if c is None:
                return CV.resb(par * CV.csz, par * CV.csz + CV.csz)
            return CV.resb(par * CV.csz + c * 1024, par * CV.csz + c * 1024 + 1024)

        def load_zc(t0):
            for c in range(8):
                S.add("sp", lambda e, c=c: e.dma_start(out=ZB.ap[:, c, 0:514], in_=ztv[:, c, t0 - 1:t0 + 513]),
                      reads=dres("zt", t0 - 1, t0 + 513, c), writes=ZB.res(c), dma="x")
                S.add("sp", lambda e, c=c: e.dma_start(out=CBt.ap[:, c, :], in_=cbv[:, c, t0:t0 + 512]),
                      reads=dres("cb", t0, t0 + 512, c), writes=CBt.res(c), dma="x")

        def load_ot(t0):
            for c in range(0, 8, 2):
                S.add("sp", lambda e, c=c: e.dma_start(out=OTt.ap[:, c:c + 2, :], in_=otv[:, c:c + 2, t0:t0 + 512]),
                      reads=dres("ot", t0, t0 + 512, [c, c + 1]), writes=OTt.res(c, 2), dma="x")

        def load_gt(t0, f):
            for gg in (f, 8 + f):
                S.add("sp", lambda e, gg=gg: e.dma_start(out=GTt.ap[:, gg, :], in_=gtv[:, gg, t0:t0 + 512]),
                      reads=dres("gt", t0, t0 + 512, gg), writes=GTt.res(gg), dma="x")

        def load_xc(t0, c):
            S.add("sp", lambda e: e.dma_start(out=X32.ap[:, c, :], in_=src[:, c, t0:t0 + 512]),
                  reads=dres(src_name, t0, t0 + 512, c), writes=X32.res(c), dma="x")

        def conv_chunk(t0, par, c):
            fixes = []
            for m, E in enumerate(cfg.edges):
                tE = E * GW
                if 0 <= tE - t0 < 512:
                    fixes.append((tE - t0, 0, tE - t0, m))
                if 0 <= tE - 1 - t0 < 512:
                    fixes.append((tE - 1 - t0, 2, tE - 1 - t0 + 2, m))
            a = c % 2
            S.add("act", lambda e: e.mul(out=CA.ap[:, a, :], in_=ZB.ap[:, c, 1:513], mul=CW.ap[:, l * 3 + 1, c:c + 1]),
                  reads=ZB.res(c) + CW.res(), writes=CA.res(a))
            S.add("dve", lambda e: e.scalar_tensor_tensor(out=CA.ap[:, a, :], in0=ZB.ap[:, c, 0:512], scalar=CW.ap[:, l * 3 + 0, c:c + 1],
                                                          in1=CA.ap[:, a, :], op0=ALU.mult, op1=ALU.add),
                  reads=ZB.res(c) + CW.res() + CA.res(a), writes=CA.res(a))
            S.add("dve", lambda e: e.scalar_tensor_tensor(out=CA.ap[:, a, :], in0=ZB.ap[:, c, 2:514], scalar=CW.ap[:, l * 3 + 2, c:c + 1],
                                                          in1=CA.ap[:, a, :], op0=ALU.mult, op1=ALU.add),
                  reads=ZB.res(c) + CW.res() + CA.res(a), writes=CA.res(a))
            for (j, wj, zc, m) in fixes:
                S.add("dve", lambda e, wj=wj, zc=zc: e.tensor_scalar(out=FX.ap[:, a, 0:1], in0=ZB.ap[:, c, zc:zc + 1],
                                                                     scalar1=CW.ap[:, l * 3 + wj, c:c + 1], scalar2=None, op0=ALU.mult),
                      reads=ZB.res(c) + CW.res(), writes=FX.res(a))
                S.add("dve", lambda e, j=j, m=m: e.scalar_tensor_tensor(out=CA.ap[:, a, j:j + 1], in0=FX.ap[:, a, 0:1], scalar=EF.ap[:, 0, m:m + 1],
                                                                        in1=CA.ap[:, a, j:j + 1], op0=ALU.mult, op1=ALU.add),
                      reads=FX.res(a) + EF.res() + CA.res(a), writes=CA.res(a))
            S.add("dve", lambda e: e.tensor_tensor(out=CV.ap[:, par, c, :], in0=CA.ap[:, a, :], in1=CBt.ap[:, c, :], op=ALU.mult),
                  reads=CA.res(a) + CBt.res(c), writes=cvres(par, c))

        t00 = tiles[0] * GW
        load_zc(t00)
        load_ot(t00)
        for f in range(8):
            load_gt(t00, f)
        for c in range(8):
            load_xc(t00, c)
        for c in range(8):
            conv_chunk(t00, 0, c)
        deferred = None
        for ti, rho0 in enumerate(tiles):
            t0 = rho0 * GW
            par = ti % 2
            nxt = tiles[ti + 1] * GW if ti + 1 < len(tiles) else None
            if nxt is not None:
                load_zc(nxt)
            for f in range(8):
                g, i = f // 4, f % 4
                b1, b2 = 2 * (f % 2), 2 * (f % 2) + 1
                mm_group(b1, [(WC.ap[:, g, k, i * 128:(i + 1) * 128], CV.ap[:, par, k, :], PS[b1][:, :]) for k in range(8)], WC.res(g) + cvres(par))
                mm_group(b2, [(WA.ap[:, g, k, i * 128:(i + 1) * 128], OTt.ap[:, k, :], PS[b2][:, :]) for k in range(8)], WA.res(g) + OTt.res())
                a = f % 2
                S.add("dve", lambda e, f=f, a=a, b1=b1: e.tensor_tensor(out=T1.ap[:, a, :], in0=GTt.ap[:, f, :], in1=PS[b1][:, :], op=ALU.mult),
                      reads=GTt.res(f) + [PSR[b1]], writes=T1.res(a))
                S.add("dve", lambda e, f=f, a=a, b2=b2: e.tensor_tensor(out=T2.ap[:, a, :], in0=GTt.ap[:, 8 + f, :], in1=PS[b2][:, :], op=ALU.mult),
                      reads=GTt.res(8 + f) + [PSR[b2]], writes=T2.res(a))
                S.add("pool", lambda e, f=f, a=a: e.tensor_tensor(out=MX.ap[:, f, :], in0=T1.ap[:, a, :], in1=T2.ap[:, a, :], op=ALU.add),
                      reads=T1.res(a) + T2.res(a), writes=MX.res(f))
                if deferred is not None:
                    deferred(f)
                if nxt is not None:
                    load_gt(nxt, f)
                    conv_chunk(nxt, 1 - par, f)
            if nxt is not None:
                load_ot(nxt)

            def terms_fn(f, out):
                g, i = f // 4, f % 4
                return [(WO.ap[:, g, k, i * 128:(i + 1) * 128], MX.ap[:, k, :], out) for k in range(8)]

            def reads_fn(f):
                return WO.res(f // 4) + MX.res()

            def st(f, t0=t0, nxt=nxt):
                S.add("pool", lambda e: e.dma_start(out=dst[:, f, t0:t0 + 512], in_=X32.ap[:, f, :]),
                      reads=X32.res(f), writes=dres(dst_name, t0, t0 + 512, f), dma="s")
                if nxt is not None:
                    load_xc(nxt, f)
            post_part1(Y32, SQ, RS, 0, terms_fn, reads_fn)

            def deferred(f, st=st):
                post_part2(X32, Y32, RS, 0, l, 3, False, f, after=st)
        for f in range(8):
            deferred(f)

    import os
    dbg = os.environ.get("K_PASSES", "")
    if dbg:
        names = dbg.split(",")
        l = 0
        if "ffn1" in names:
            ffn_pass(l, w1i, w1o, 0, 1, fm(xin), "xin", fm(XR[0]), "xr0", cfg.in_rows(l), 0)
        if "mixa" in names:
            mixa_pass(l, fm(xin), "xin", cfg.in_rows(l))
        if "att" in names:
            att_pass(l)
        if "mixb" in names:
            mixb_pass(l, fm(xin), "xin", fm(XR[1]), "xr1")
        arena.reset()
        TB = arena.alloc([8, 512], F32)
        dsrc = {"ffn1": XR[0], "mixa": ZT, "att": ZT, "mixb": XR[1]}[names[-1]]
        dn = {"ffn1": "xr0", "mixa": "zt", "att": "zt", "mixb": "xr1"}[names[-1]]
        for rho0 in range(cfg.H, cfg.H + cfg.RB, 8):
            t0 = rho0 * GW
            S.add("sp", lambda e, t0=t0: e.dma_start(out=TB.ap, in_=fm(dsrc)[:, :, t0:t0 + 512]), reads=dres(dn, t0, t0 + 512), writes=TB.res(), dma="x")
            S.add("pool", lambda e, t0=t0: e.dma_start(out=fm(yout)[:, :, t0 - cfg.H * GW:t0 - cfg.H * GW + 512], in_=TB.ap), reads=TB.res(),
                  writes=dres("yout", t0 - cfg.H * GW, t0 - cfg.H * GW + 512), dma="s")
        S.add("sp", lambda e: None, reads=list(dres_tab["yout"].values()))
        S.emit()
        return nc
    for l in range(NL):
        if l == 0:
            src, sname = fm(xin), "xin"
        else:
            src, sname = fm(XR[1]), "xr1"
        ffn_pass(l, w1i, w1o, 0, 1, src, sname, fm(XR[0]), "xr0", cfg.in_rows(l), 0)
        mixa_pass(l, fm(XR[0]), "xr0", cfg.in_rows(l))
        att_pass(l)
        mixb_pass(l, fm(XR[0]), "xr0", fm(XR[1]), "xr1")
        if l == NL - 1:
            ffn_pass(l, w2i, w2o, 4, 5, fm(XR[1]), "xr1", fm(yout), "yout", cfg.out_rows(l), cfg.H * GW)
        else:
            ffn_pass(l, w2i, w2o, 4, 5, fm(XR[1]), "xr1", fm(XR[1]), "xr1", cfg.out_rows(l), 0)
    S.add("sp", lambda e: None, reads=list(dres_tab["yout"].values()))
    S.emit()
    return nc


def host_tables(cfg, c):
    RT, RB, H = cfg.RT, cfg.RB, cfg.H
    starts = np.concatenate([[0], np.cumsum(cfg.seqs)])
    total = int(starts[-1])

    def true_window(g):
        if g < 0 or g >= total:
            return (g - 4, g + 3)
        si = int(np.searchsorted(starts, g, side="right") - 1)
        gs, L = int(starts[si]), cfg.seqs[si]
        rs = int(np.clip(g - gs - 4, 0, L - 8))
        return (gs + rs, gs + rs + 7)

    mt = np.zeros((128, RT, 6), np.float32)
    for rho in range(RT):
        g = RB * c + rho - H
        lo, hi = true_window(g)
        for s, ps in enumerate(cfg.slots(rho)):
            for kr in range(2):
                gk = RB * c + ps + kr - H
                if not (lo <= gk <= hi):
                    mt[kr * 64:(kr + 1) * 64, rho, s] = NEG
    ef = np.zeros((128, 8), np.float32)
    for m, E in enumerate(cfg.edges):
        g = RB * c + E - H
        if g <= 0 or g >= total or g in set(int(x) for x in starts):
            ef[:, m] = -1.0
    return mt.reshape(128, RT * 6), ef


def bias_table(rpb_l):
    kc = np.arange(64)[:, None]
    qc = np.arange(64)[None, :]
    cs = np.clip(qc - 8, 0, 48)
    ok = (kc >= cs) & (kc < cs + 16)
    dci = np.clip(kc - qc, -15, 15) + 15
    B = np.full((NH, 17, 64, 64), NEG, np.float32)
    G = rpb_l[:, :, dci]
    G = np.where(ok[None, None], G, np.float32(NEG))
    B[:, 1:16] = G
    T = np.empty((2, 64, 16, 4, 8, 64), np.float32)
    Bh = B.reshape(8, 4, 17, 64, 64)
    for kr in range(2):
        T[kr] = np.transpose(Bh[:, :, kr:kr + 16], (3, 2, 1, 0, 4))
    return T.reshape(128, 16 * 2048)


def prepare(cfg, inp):
    NL = cfg.NL
    xs = [np.asarray(inp["x_prompt"], np.float32), np.asarray(inp["x_sample"], np.float32)]
    glob = np.concatenate([x.reshape(-1, D) for x in xs], axis=0)
    total_rows = sum(cfg.seqs)
    assert glob.shape[0] == total_rows * GW
    pad = np.zeros((cfg.H * GW, D), np.float32)
    gp = np.concatenate([pad, glob, pad], axis=0)

    def pvec(a, inner):
        a = np.asarray(a, np.float32)
        sh = a.shape[:-1]
        a = a.reshape(*sh, inner, 128)
        a = np.moveaxis(a, -1, 0)
        return np.ascontiguousarray(a.reshape(128, -1))

    gains = np.stack([np.asarray(inp[k], np.float32)[:NL] for k in
                      ("g_ffn1_pre", "g_ffn1_post", "g_mix_pre", "g_mix_post", "g_ffn2_pre", "g_ffn2_post")], axis=1)
    common = {
        "gvec": pvec(gains, 8),
        "bgate": pvec(np.asarray(inp["b_mix_gate"])[:NL], 16),
        "convw": pvec(np.asarray(inp["conv_w"])[:NL], 8),
        "ttab": np.stack([bias_table(np.asarray(inp["na_rpb"], np.float32)[l]) for l in range(NL)], axis=0),
        "ident": np.eye(128, dtype=np.float32),
        "onesm": np.full((128, 128), 1.0 / D, np.float32),
    }
    for k in ("w_ffn1_in", "w_ffn1_out", "w_mix_in", "w_conv_branch", "w_att_branch", "w_mix_out", "w_ffn2_in", "w_ffn2_out"):
        common[k] = np.ascontiguousarray(np.asarray(inp[k], np.float32)[:NL])
    maps = []
    for c in range(cfg.ncores):
        r0 = cfg.RB * c
        xin = np.ascontiguousarray(gp[r0 * GW:(r0 + cfg.RT) * GW].T)
        mt, ef = host_tables(cfg, c)
        m = dict(common)
        m["xin"] = xin
        m["mtab"] = mt
        m["eflag"] = ef
        maps.append(m)
    return maps


def run(cfg, inp):
    nc = build_program(cfg)
    maps = prepare(cfg, inp)
    import os
    ncr = int(os.environ.get("K_NCORES", cfg.ncores))
    if os.environ.get("K_TRACE"):
        res = run_bass_kernel_spmd(nc, maps[:ncr], core_ids=list(range(ncr)), trace=True)
        print("EXEC_TIME_NS", res.exec_time_ns)
    else:
        res = run_bass_kernel_spmd(nc, maps[:ncr], core_ids=list(range(ncr)))
    ys = [np.asarray(r["yout"], np.float32).T for r in res.results]
    ys = ys + [ys[0]] * (cfg.ncores - ncr)
    glob = np.concatenate(ys, axis=0)
    np_ = int(np.asarray(inp["x_prompt"]).shape[0] * np.asarray(inp["x_prompt"]).shape[1])
    yp = glob[:np_].reshape(np.asarray(inp["x_prompt"]).shape)
    ysm = glob[np_:].reshape(np.asarray(inp["x_sample"]).shape)
    return (np.ascontiguousarray(yp), np.ascontiguousarray(ysm))


def kernel(**inputs):
    cfg = Cfg()
    return run(cfg, inputs)
```

```python
import numpy as np
import concourse.bass as bass
import concourse.mybir as mybir
from concourse.bass_utils import run_bass_kernel_spmd

F32 = mybir.dt.float32
BF16 = mybir.dt.bfloat16
AF = mybir.ActivationFunctionType
ALU = mybir.AluOpType

COMPUTE = ("pe", "act", "dve", "pool")
ENGS = ("pe", "act", "dve", "pool", "sp")
KROT = 4
KDMA = 12
NEG = -1.0e30

D = 1024
DFF = 2816
DIN = 8192
GW = 64
NH = 32
RMS_EPS = 1e-6
QSCALE = 32 ** -0.5


class Res:
    __slots__ = ("w", "rs", "excl")

    def __init__(self, excl=False):
        self.w = None
        self.rs = []
        self.excl = excl


class Op:
    __slots__ = ("eng", "fn", "deps", "dma", "xdep", "ms", "idx")

    def __init__(self, eng, fn, dma):
        self.eng = eng
        self.fn = fn
        self.deps = []
        self.dma = dma
        self.xdep = False
        self.ms = None
        self.idx = 0


class Sched:
    def __init__(self, nc):
        self.nc = nc
        self.eng_ops = {e: [] for e in ENGS}
        self.n = 0
        self.dma_count = {}

    def add(self, eng, fn, reads=(), writes=(), dma=None):
        op = Op(eng, fn, dma)
        op.idx = self.n
        self.n += 1
        ex = [r for r in reads if r.excl]
        if ex:
            reads = [r for r in reads if not r.excl]
            writes = list(writes) + ex
        deps = {}
        for r in reads:
            if r.w is not None:
                deps[id(r.w)] = r.w
        for w in writes:
            if w.w is not None:
                deps[id(w.w)] = w.w
            for rd in w.rs:
                deps[id(rd)] = rd
        wid = set(id(w) for w in writes)
        for w in writes:
            w.w = op
            w.rs = []
        for r in reads:
            if id(r) not in wid:
                r.rs.append(op)
        for d in deps.values():
            if d is op:
                continue
            if d.dma is None and d.eng == eng and eng == "pe":
                continue
            op.deps.append(d)
            d.xdep = True
        if dma is not None:
            c = self.dma_count.get(dma, 0)
            op.ms = c
            self.dma_count[dma] = c + 1
        self.eng_ops[eng].append(op)
        return op

    def emit(self):
        nc = self.nc
        for e in COMPUTE:
            m = 0
            for op in self.eng_ops[e]:
                if op.dma is None and op.xdep:
                    op.ms = m
                    m += 1
        sems = {}
        for e in COMPUTE:
            sems[e] = [nc.alloc_semaphore(f"s_{e}{k}") for k in range(KROT)]
        for c in self.dma_count:
            sems["dma_" + c] = [nc.alloc_semaphore(f"d_{c}{k}") for k in range(KDMA)]
        sched = self

        def run_engine(ename):
            def body(eng):
                done_ms = {e: -1 for e in COMPUTE}
                waited = {}
                for op in sched.eng_ops[ename]:
                    for d in sorted(op.deps, key=lambda o: o.idx):
                        if d.dma is None:
                            if done_ms[d.eng] >= d.ms:
                                continue
                            done_ms[d.eng] = d.ms
                            s = sems[d.eng][d.ms % KROT]
                            v = d.ms // KROT + 1
                        else:
                            key = ("dma_" + d.dma, d.ms % KDMA)
                            s = sems[key[0]][key[1]]
                            v = 16 * (d.ms // KDMA + 1)
                            if waited.get(key, 0) >= v:
                                continue
                            waited[key] = v
                        eng.wait_ge(s, v)
                    if op.dma is not None and op.ms >= KDMA:
                        key = ("dma_" + op.dma, op.ms % KDMA)
                        v = 16 * (op.ms // KDMA)
                        if waited.get(key, 0) < v:
                            waited[key] = v
                            eng.wait_ge(sems[key[0]][key[1]], v)
                    last = op.fn(eng)
                    if op.dma is not None:
                        last.then_inc(sems["dma_" + op.dma][op.ms % KDMA], 16)
                    elif op.xdep:
                        last.then_inc(sems[op.eng][op.ms % KROT], 1)
            return body

        with nc.Block() as block:
            block.tensor(run_engine("pe"))
            block.scalar(run_engine("act"))
            block.vector(run_engine("dve"))
            block.gpsimd(run_engine("pool"))
            block.sync(run_engine("sp"))


BLK = 512


class Buf:
    def __init__(self, arena, off, shape, dtype):
        self.arena = arena
        self.off = off
        self.shape = list(shape)
        self.dtype = dtype
        self.esz = 4 if dtype == F32 else 2
        n = int(np.prod(shape))
        self.nbytes = n * self.esz
        ap = arena.t[:, off // 2:(off + self.nbytes) // 2]
        if dtype == F32:
            ap = ap.bitcast(F32)
        if len(shape) == 2:
            ap = ap.rearrange("p (a b) -> p a b", a=shape[0])
        elif len(shape) == 3:
            ap = ap.rearrange("p (a b c) -> p a b c", a=shape[0], b=shape[1])
        elif len(shape) == 4:
            ap = ap.rearrange("p (a b c d) -> p a b c d", a=shape[0], b=shape[1], c=shape[2])
        self.ap = ap
        self.csz = self.nbytes // shape[0]

    def res(self, i=None, n=1):
        if i is None:
            lo, hi = self.off, self.off + self.nbytes
        else:
            lo = self.off + i * self.csz
            hi = lo + n * self.csz
        return self.arena.blocks[lo // BLK:(hi + BLK - 1) // BLK]

    def resb(self, lo, hi):
        lo += self.off
        hi += self.off
        return self.arena.blocks[lo // BLK:(hi + BLK - 1) // BLK]


class Arena:
    def __init__(self, nc, nbytes):
        self.t = nc.alloc_sbuf_tensor("arena", [128, nbytes // 2], BF16)
        self.nbytes = nbytes
        self.blocks = [Res() for _ in range(nbytes // BLK)]
        self.base = 0
        self.cur = 0

    def alloc(self, shape, dtype):
        b = Buf(self, self.cur, shape, dtype)
        self.cur += (b.nbytes + BLK - 1) // BLK * BLK
        assert self.cur <= self.nbytes, f"arena overflow {self.cur} > {self.nbytes}"
        return b

    def fix(self):
        self.base = self.cur

    def reset(self):
        self.cur = self.base


class Cfg:
    def __init__(self, NL=4, RB=96, ES=32, HL=4, seqs=(64, 64, 64, 64, 256, 256), ncores=8):
        self.NL = NL
        self.RB = RB
        self.ES = ES
        self.HL = HL
        self.seqs = list(seqs)
        self.ncores = ncores
        self.H = HL * NL
        self.RT = RB + 2 * self.H
        self.NT = self.RT * GW
        self.edges = [self.H + m * ES for m in range(RB // ES + 1)]
        assert sum(seqs) == RB * ncores
        assert RB % 8 == 0 and HL == 4

    def in_rows(self, l):
        return (self.HL * l, self.RT - self.HL * l)

    def out_rows(self, l):
        return (self.HL * (l + 1), self.RT - self.HL * (l + 1))

    def window(self, rho):
        for E in self.edges:
            if 0 <= rho - E <= 3:
                return (rho - 4, E + 7)
            if 1 <= E - rho <= 3:
                return (E - 8, rho + 3)
        return (rho - 4, rho + 3)

    def slots(self, rho):
        lo, hi = self.window(rho)
        lo -= lo % 2
        return list(range(lo, hi + 1, 2))


def build_program(cfg):
    NL, RT, NT = cfg.NL, cfg.RT, cfg.NT
    nc = bass.Bass("TRN2", target_bir_lowering=False)

    def din(name, shape):
        return nc.dram_tensor(name, list(shape), F32, kind="ExternalInput")

    xin = din("xin", [D, NT])
    w1i = din("w_ffn1_in", [NL, D, 2 * DFF])
    w1o = din("w_ffn1_out", [NL, DFF, D])
    wmi = din("w_mix_in", [NL, D, DIN])
    wcb = din("w_conv_branch", [NL, D, D])
    wab = din("w_att_branch", [NL, D, D])
    wmo = din("w_mix_out", [NL, D, D])
    w2i = din("w_ffn2_in", [NL, D, 2 * DFF])
    w2o = din("w_ffn2_out", [NL, DFF, D])
    gvec_d = din("gvec", [128, NL * 6 * 8])
    bgate_d = din("bgate", [128, NL * 16])
    convw_d = din("convw", [128, NL * 3 * 8])
    ttab_d = din("ttab", [NL, 128, 16 * 2048])
    mtab_d = din("mtab", [128, RT * 6])
    eflag_d = din("eflag", [128, 8])
    ident_d = din("ident", [128, 128])
    onesm_d = din("onesm", [128, 128])
    yout = nc.dram_tensor("yout", [D, cfg.RB * GW], F32, kind="ExternalOutput")

    def dscr(name, shape, dt):
        return nc.dram_tensor(name, list(shape), dt, kind="Internal")

    XR = [dscr("xr0", [D, NT], F32), dscr("xr1", [D, NT], F32)]
    QT = dscr("qt", [D, NT], BF16)
    KT = dscr("kt", [D, NT], BF16)
    VV = dscr("vv", [NT, D], BF16)
    OT = dscr("ot", [D, NT], BF16)
    ZT = dscr("zt", [D, NT], F32)
    CB = dscr("cb", [D, NT], F32)
    GT = dscr("gt", [2 * D, NT], F32)

    def fm(t):
        return t.ap().rearrange("(c p) t -> p c t", p=128)

    dres_tab = {}

    NCH = {"gt": 16, "vv": 1}

    def dres(name, lo, hi, c=None):
        tab = dres_tab.setdefault(name, {})
        n = NCH.get(name, 8)
        cs = range(n) if c is None else ([c] if isinstance(c, int) else c)
        out = []
        for cc in cs:
            for b in range(lo // 256, (hi + 255) // 256):
                key = (cc, b)
                if key not in tab:
                    tab[key] = Res()
                out.append(tab[key])
        return out

    S = Sched(nc)
    arena = Arena(nc, 207 * 1024)
    PSA = nc.alloc_psum_tensor("psa", [128, 4096], F32)
    PS = [PSA[:, 512 * i:512 * (i + 1)] for i in range(8)]
    PSR = [Res(excl=True) for _ in range(8)]

    class View:
        def __init__(self, ap, base):
            self.ap = ap
            self.base = base

        def res(self, *a_, **k_):
            return self.base.res()

    C1 = arena.alloc([NL * 12, 8], F32)
    oBG, oCW = 0, NL * 16
    oEF = oCW + NL * 24
    oEPS = oEF + 8
    assert oEPS + 8 <= 256
    C2 = arena.alloc([1, 256], F32)
    C3 = arena.alloc([1, 512], BF16)
    arena.fix()
    GV = View(C1.ap[:, 0:NL * 6, :], C1)
    GH = View(C1.ap[:, NL * 6:NL * 12, :], C1)
    BG = View(C2.ap[:, 0, oBG:oBG + NL * 16].rearrange("p (a b) -> p a b", b=16), C2)
    CW = View(C2.ap[:, 0, oCW:oCW + NL * 24].rearrange("p (a b) -> p a b", b=8), C2)
    EF = View(C2.ap[:, 0, oEF:oEF + 8].rearrange("p (a b) -> p a b", a=1), C2)
    EPS = View(C2.ap[:, 0, oEPS:oEPS + 8].rearrange("p (a b) -> p a b", a=1), C2)
    IDN = View(C3.ap[:, 0, 0:128].rearrange("p (a b) -> p a b", a=1), C3)
    ONM = View(C3.ap[:, 0, 128:256].rearrange("p (a b) -> p a b", a=1), C3)
    ON32 = View(C3.ap[:, 0, 256:288].rearrange("p (a b) -> p a b", a=1), C3)

    def ld(buf, src, eng="sp", cls="c"):
        S.add(eng, lambda e: e.dma_start(out=buf.ap, in_=src), writes=buf.res(), dma=cls)

    ld(GV, gvec_d.ap().rearrange("p (a b) -> p a b", b=8))
    ld(BG, bgate_d.ap().rearrange("p (a b) -> p a b", b=16))
    ld(CW, convw_d.ap().rearrange("p (a b) -> p a b", b=8))
    ld(EF, eflag_d.ap().rearrange("p (a b) -> p a b", a=1))
    ld(IDN, ident_d.ap().rearrange("p (a b) -> p a b", a=1), eng="pool", cls="w")
    ld(ONM, onesm_d.ap().rearrange("p (a b) -> p a b", a=1), eng="pool", cls="w")
    S.add("dve", lambda e: e.memset(EPS.ap, RMS_EPS), writes=EPS.res())
    S.add("dve", lambda e: e.memset(ON32.ap, 1.0), writes=ON32.res())
    S.add("dve", lambda e: e.tensor_scalar(out=GH.ap, in0=GV.ap, scalar1=0.5, scalar2=None, op0=ALU.mult),
          reads=GV.res(), writes=GH.res())
    ident = IDN.ap[:, 0, :]
    onesm = ONM.ap[:, 0, :]
    ones32 = ON32.ap[:, 0, :]
    eps = EPS.ap[:, 0, 0:1]

    def gv(l, kind, c, half=False):
        return (GH if half else GV).ap[:, l * 6 + kind, c:c + 1]

    def mm_group(bank, terms, reads, extra_writes=()):
        def fn(e):
            last = None
            n = len(terms)
            for i, (lt, rh, out) in enumerate(terms):
                last = e.matmul(out, lhsT=lt, rhs=rh, start=(i == 0), stop=(i == n - 1))
            return last
        S.add("pe", fn, reads=reads, writes=[PSR[bank]] + list(extra_writes))

    def pn_square(X32, SQ, c):
        S.add("act", lambda e: e.activation(out=SQ.ap[:, c % 2, :], in_=X32.ap[:, c, :], func=AF.Square),
              reads=X32.res(c), writes=SQ.res(c % 2))

    def pn_stat(SQ, c, stbank):
        S.add("pe", lambda e: e.matmul(PS[stbank][:, :], lhsT=onesm, rhs=SQ.ap[:, c % 2, :], start=(c == 0), stop=(c == 7)),
              reads=SQ.res(c % 2) + ONM.res(), writes=[PSR[stbank]])

    def pn_rstd(RS, r, stbank):
        S.add("act", lambda e: e.activation(out=RS.ap[:, r, :], in_=PS[stbank][:, :], func=AF.Sqrt, bias=eps, scale=1.0),
              reads=[PSR[stbank]] + EPS.res(), writes=RS.res(r))
        S.add("dve", lambda e: e.reciprocal(out=RS.ap[:, r, :], in_=RS.ap[:, r, :]), reads=RS.res(r), writes=RS.res(r))

    def pn_apply(X32, XN, RS, r, l, kind):
        for c in range(8):
            S.add("dve", lambda e, c=c: e.scalar_tensor_tensor(out=XN.ap[:, c, :], in0=X32.ap[:, c, :], scalar=gv(l, kind, c),
                                                               in1=RS.ap[:, r, :], op0=ALU.mult, op1=ALU.mult),
                  reads=X32.res(c) + RS.res(r) + GV.res(), writes=XN.res(c))

    def prenorm(X32, XN, SQ, RS, l, kind, stbank=6):
        for c in range(8):
            pn_square(X32, SQ, c)
            pn_stat(SQ, c, stbank)
        pn_rstd(RS, 0, stbank)
        pn_apply(X32, XN, RS, 0, l, kind)

    def post_part1(Y32, SQ, RS, r, proj_terms_fn, proj_reads_fn, ybanks=(4, 5), stbank=6):
        for f in range(8):
            b = ybanks[f % 2]
            mm_group(b, proj_terms_fn(f, PS[b][:, :]), proj_reads_fn(f))
            if f > 0:
                pn_stat(SQ, f - 1, stbank)
            S.add("act", lambda e, f=f, b=b: e.activation(out=Y32.ap[:, f, :], in_=PS[b][:, :], func=AF.Copy), reads=[PSR[b]], writes=Y32.res(f))
            pn_square(Y32, SQ, f)
        pn_stat(SQ, 7, stbank)
        pn_rstd(RS, r, stbank)

    def post_part2(XR, Y32, RS, r, l, kind, half, f, after=None):
        S.add("pool", lambda e: e.tensor_tensor(out=Y32.ap[:, f, :], in0=Y32.ap[:, f, :], in1=RS.ap[:, r, :], op=ALU.mult),
              reads=Y32.res(f) + RS.res(r), writes=Y32.res(f))
        S.add("dve", lambda e: e.scalar_tensor_tensor(out=XR.ap[:, f, :], in0=Y32.ap[:, f, :], scalar=gv(l, kind, f, half),
                                                      in1=XR.ap[:, f, :], op0=ALU.mult, op1=ALU.add),
              reads=Y32.res(f) + XR.res(f) + GH.res() + GV.res(), writes=XR.res(f))
        if after is not None:
            after(f)

    def load_x(X32, src, src_name, t0):
        for c in range(8):
            S.add("sp", lambda e, c=c: e.dma_start(out=X32.ap[:, c, :], in_=src[:, c, t0:t0 + 512]),
                  reads=dres(src_name, t0, t0 + 512, c), writes=X32.res(c), dma="x")

    def ffn_pass(l, w_in, w_out, kpre, kpost, src, src_name, dst, dst_name, rows, dst_off):
        arena.reset()
        WIN = arena.alloc([11, 8, 512], BF16)
        WOUT = arena.alloc([22, 1024], BF16)
        X32 = arena.alloc([8, 512], F32)
        XY = arena.alloc([3, 8, 512], BF16)
        XNs = [Buf(arena, XY.off, [8, 512], BF16), Buf(arena, XY.off + 16384, [8, 512], BF16)]
        Y32s = [Buf(arena, XY.off, [8, 512], F32), Buf(arena, XY.off + 8192, [8, 512], F32)]
        A = arena.alloc([22, 512], BF16)
        XRA = Buf(arena, A.off + 6 * 1024, [8, 512], F32)
        SQ = arena.alloc([2, 512], BF16)
        RS = arena.alloc([2, 512], F32)
        SG = arena.alloc([1, 512], F32)
        wiv = w_in.ap()[l].rearrange("(k p) n -> p k n", p=128)
        wov = w_out.ap()[l].rearrange("(k p) n -> p k n", p=128)
        for g in range(11):
            S.add("pool", lambda e, g=g: e.dma_start(out=WIN.ap[:, g], in_=wiv[:, :, g * 512:(g + 1) * 512]),
                  writes=WIN.res(g), dma="w")
        for q in range(6):
            k0, k1 = q * 4, min(22, q * 4 + 4)
            S.add("pool", lambda e, k0=k0, k1=k1: e.dma_start(out=WOUT.ap[:, k0:k1, :], in_=wov[:, k0:k1, :]),
                  writes=WOUT.res(k0, k1 - k0), dma="w")
        ra, rb = rows
        tiles = list(range(ra, rb, 8))
        load_x(X32, src, src_name, tiles[0] * GW)
        for c in range(8):
            pn_square(X32, SQ, c)
            pn_stat(SQ, c, 7)
        pn_rstd(RS, 0, 7)
        pn_apply(X32, XNs[0], RS, 0, l, kpre)
        deferred = None
        for ti, rho0 in enumerate(tiles):
            t0 = rho0 * GW
            nxt = tiles[ti + 1] * GW if ti + 1 < len(tiles) else None
            XN, YX = XNs[ti % 2], Y32s[ti % 2]
            for j in range(22):
                for (n, b) in ((j, 2 * (j % 2)), (22 + j, 2 * (j % 2) + 1)):
                    g, i = n // 4, n % 4
                    terms = [(WIN.ap[:, g, k, i * 128:(i + 1) * 128], XN.ap[:, k, :], PS[b][:, :]) for k in range(8)]
                    mm_group(b, terms, WIN.res(g) + XN.res())
                bg, bu = 2 * (j % 2), 2 * (j % 2) + 1
                S.add("act", lambda e, j=j, bg=bg: e.activation(out=SG.ap[:, 0, :], in_=PS[bg][:, :], func=AF.Silu),
                      reads=[PSR[bg]], writes=SG.res(0))
                S.add("dve", lambda e, j=j, bu=bu: e.tensor_tensor(out=A.ap[:, j, :], in0=SG.ap[:, 0, :], in1=PS[bu][:, :], op=ALU.mult),
                      reads=SG.res(0) + [PSR[bu]], writes=A.res(j))
                if deferred is not None and j < 4:
                    deferred(2 * j)
                    deferred(2 * j + 1)
                if nxt is not None:
                    if j == 3:
                        load_x(X32, src, src_name, nxt)
                    if 10 <= j < 18:
                        pn_square(X32, SQ, j - 10)
                    if 11 <= j < 19:
                        pn_stat(SQ, j - 11, 7)
                    if j == 19:
                        pn_rstd(RS, 0, 7)
                    if j == 20:
                        pn_apply(X32, XNs[(ti + 1) % 2], RS, 0, l, kpre)

            def terms_fn(f, out):
                return [(WOUT.ap[:, kk, f * 128:(f + 1) * 128], A.ap[:, kk, :], out) for kk in range(22)]

            def reads_fn(f):
                return WOUT.res() + A.res()
            post_part1(YX, SQ, RS, 1, terms_fn, reads_fn)
            for f in range(8):
                S.add("sp", lambda e, f=f, t0=t0: e.dma_start(out=XRA.ap[:, f, :], in_=src[:, f, t0:t0 + 512]),
                      reads=dres(src_name, t0, t0 + 512, f), writes=XRA.res(f), dma="x")
            d0 = t0 - dst_off

            def st(f, d0=d0):
                S.add("pool", lambda e: e.dma_start(out=dst[:, f, d0:d0 + 512], in_=XRA.ap[:, f, :]),
                      reads=XRA.res(f), writes=dres(dst_name, d0, d0 + 512, f), dma="s")
            def deferred(f, YX=YX, st=st):
                post_part2(XRA, YX, RS, 1, l, kpost, True, f, after=st)
        for f in range(8):
            deferred(f)

    def mixa_pass(l, src, src_name, rows):
        arena.reset()
        WIN = arena.alloc([16, 8, 512], BF16)
        X32 = arena.alloc([8, 512], F32)
        XN = arena.alloc([8, 512], BF16)
        SQ = arena.alloc([2, 512], BF16)
        RS = arena.alloc([1, 512], F32)
        STG = arena.alloc([4, 512], F32)
        STB = arena.alloc([4, 512], BF16)
        CCT = arena.alloc([2, 512], F32)
        VS = arena.alloc([2, 1024], BF16)
        wiv = wmi.ap()[l].rearrange("(k p) n -> p k n", p=128)
        for g in range(16):
            S.add("pool", lambda e, g=g: e.dma_start(out=WIN.ap[:, g], in_=wiv[:, :, g * 512:(g + 1) * 512]),
                  writes=WIN.res(g), dma="w")
        cbv, ztv, qtv, ktv, gtv = fm(CB), fm(ZT), fm(QT), fm(KT), fm(GT)
        ra, rb = rows
        cnt = {"b": 0, "s": 0, "sb": 0}

        def nb():
            cnt["b"] = (cnt["b"] + 1) % 4
            return cnt["b"]

        for rho0 in range(ra, rb, 8):
            t0 = rho0 * GW
            load_x(X32, src, src_name, t0)
            prenorm(X32, XN, SQ, RS, l, 2)

            def proj(n):
                b = nb()
                g, i = n // 4, n % 4
                terms = [(WIN.ap[:, g, k, i * 128:(i + 1) * 128], XN.ap[:, k, :], PS[b][:, :]) for k in range(8)]
                mm_group(b, terms, WIN.res(g) + XN.res())
                return b

            def store(view, name, c, stg, r, t0=t0):
                S.add("pool", lambda e: e.dma_start(out=view[:, c, t0:t0 + 512], in_=stg.ap[:, r, :]),
                      reads=stg.res(r), writes=dres(name, t0, t0 + 512, c), dma="s")

            for c in range(8):
                b = proj(c)
                r = cnt["s"] = (cnt["s"] + 1) % 4
                S.add("act", lambda e, b=b, r=r: e.activation(out=STG.ap[:, r, :], in_=PS[b][:, :], func=AF.Copy),
                      reads=[PSR[b]], writes=STG.res(r))
                store(cbv, "cb", c, STG, r)
            for c in range(8):
                b = proj(8 + c)
                S.add("dve", lambda e, b=b, c=c: e.tensor_copy(out=CCT.ap[:, c % 2, :], in_=PS[b][:, :]),
                      reads=[PSR[b]], writes=CCT.res(c % 2))
                b = proj(16 + c)
                r = cnt["s"] = (cnt["s"] + 1) % 4
                S.add("dve", lambda e, b=b, c=c, r=r: e.tensor_tensor(out=STG.ap[:, r, :], in0=CCT.ap[:, c % 2, :], in1=PS[b][:, :], op=ALU.mult),
                      reads=[PSR[b]] + CCT.res(c % 2), writes=STG.res(r))
                store(ztv, "zt", c, STG, r)
            for c in range(8):
                b = proj(24 + c)
                r = cnt["sb"] = (cnt["sb"] + 1) % 4
                S.add("act", lambda e, b=b, r=r: e.mul(out=STB.ap[:, r, :], in_=PS[b][:, :], mul=QSCALE),
                      reads=[PSR[b]], writes=STB.res(r))
                store(qtv, "qt", c, STB, r)
            for c in range(8):
                b = proj(32 + c)
                r = cnt["sb"] = (cnt["sb"] + 1) % 4
                S.add("dve", lambda e, b=b, r=r: e.tensor_copy(out=STB.ap[:, r, :], in_=PS[b][:, :]),
                      reads=[PSR[b]], writes=STB.res(r))
                store(ktv, "kt", c, STB, r)
            for tb in range(4):
                for hf in range(2):
                    b = nb()
                    terms = [(XN.ap[:, k, tb * 128:(tb + 1) * 128], WIN.ap[:, 10 + hf, k, :], PS[b][:, :]) for k in range(8)]
                    mm_group(b, terms, WIN.res(10 + hf) + XN.res())
                    if hf == 0:
                        S.add("act", lambda e, b=b, tb=tb: e.activation(out=VS.ap[:, tb % 2, 0:512], in_=PS[b][:, :], func=AF.Copy),
                              reads=[PSR[b]], writes=VS.res(tb % 2))
                    else:
                        S.add("dve", lambda e, b=b, tb=tb: e.tensor_copy(out=VS.ap[:, tb % 2, 512:1024], in_=PS[b][:, :]),
                              reads=[PSR[b]], writes=VS.res(tb % 2))
                tk = t0 + tb * 128
                S.add("pool", lambda e, tk=tk, tb=tb: e.dma_start(out=VV.ap()[tk:tk + 128, :], in_=VS.ap[:, tb % 2, :]),
                      reads=VS.res(tb % 2), writes=dres("vv", tk, tk + 128), dma="s")
            for cg in range(16):
                b = proj(48 + cg)
                r = cnt["s"] = (cnt["s"] + 1) % 4
                S.add("act", lambda e, b=b, r=r, cg=cg: e.activation(out=STG.ap[:, r, :], in_=PS[b][:, :], func=AF.Sigmoid,
                                                                     bias=BG.ap[:, l, cg:cg + 1], scale=1.0),
                      reads=[PSR[b]] + BG.res(), writes=STG.res(r))
                store(gtv, "gt", cg, STG, r)

    def att_pass(l):
        arena.reset()
        TT = arena.alloc([16, 4, 512], BF16)
        KR = arena.alloc([8, 4, 512], BF16)
        VR = arena.alloc([4, 4, 1024], BF16)
        QB = arena.alloc([2, 8, 512], BF16)
        PT = arena.alloc([4, 4, 512], BF16)
        PTF = Buf(arena, PT.off, [4, 2048], BF16)
        TTF = Buf(arena, TT.off, [16, 2048], BF16)
        OS = arena.alloc([2, 8, 512], BF16)
        DI = arena.alloc([2, 512], F32)
        MT = arena.alloc([RT, 6], F32)
        S.add("sp", lambda e: e.dma_start(out=MT.ap, in_=mtab_d.ap().rearrange("p (a b) -> p a b", b=6)), writes=MT.res(), dma="x")
        ttv = ttab_d.ap()[l].rearrange("p (a b c) -> p a b c", a=16, b=4)
        for a_ in range(0, 16, 2):
            S.add("pool", lambda e, a_=a_: e.dma_start(out=TT.ap[:, a_:a_ + 2], in_=ttv[:, a_:a_ + 2]),
                  writes=TT.res(a_, 2), dma="w")
        for di in range(16):
            S.add("act", lambda e, di=di: e.activation(out=TT.ap[:, di], in_=TT.ap[:, di], func=AF.Exp),
                  reads=TT.res(di), writes=TT.res(di))
        ilo, ihi = cfg.in_rows(l)
        olo, ohi = cfg.out_rows(l)
        rbase = olo - 8
        ktv, qtv, otv = fm(KT), fm(QT), fm(OT)
        ntile = (ohi - olo) // 8

        def kres(sl):
            return [x for c in range(8) for x in KR.resb(c * KR.csz + sl * 1024, c * KR.csz + sl * 1024 + 1024)]

        def load_seg(j):
            r0 = max(rbase + 8 * j, ilo)
            r1 = min(rbase + 8 * j + 8, ihi)
            if r1 <= r0:
                return
            sl = j % 4
            o0 = (r0 - (rbase + 8 * j)) * GW
            n = (r1 - r0) * GW
            t0 = r0 * GW
            S.add("sp", lambda e: e.dma_start(out=KR.ap[:, :, sl, o0:o0 + n], in_=ktv[:, :, t0:t0 + n]),
                  reads=dres("kt", t0, t0 + n), writes=kres(sl), dma="x")
            p0 = o0 // 128
            S.add("sp", lambda e: e.dma_start(out=VR.ap[:, sl, p0:p0 + n // 128, :],
                                              in_=VV.ap()[t0:t0 + n, :].rearrange("(a p) f -> p a f", p=128)),
                  reads=dres("vv", t0, t0 + n), writes=VR.res(sl), dma="x")

        def load_q(i):
            t0 = (olo + 8 * i) * GW
            S.add("sp", lambda e: e.dma_start(out=QB.ap[:, i % 2], in_=qtv[:, :, t0:t0 + 512]),
                  reads=dres("qt", t0, t0 + 512), writes=QB.res(i % 2), dma="x")

        pend = []

        def flush(keep=0):
            while len(pend) > keep:
                for f_ in pend.pop(0):
                    f_()

        def emit_pv(sl, pr, pts, first, last, ob, db):
            def fn(e):
                lastm = None
                for c in range(8):
                    for hp in range(4):
                        e.matmul(PS[ob][32 * hp:32 * hp + 32, c * 64:(c + 1) * 64],
                                 lhsT=VR.ap[:, sl, pr, c * 128 + 32 * hp:c * 128 + 32 * hp + 32],
                                 rhs=PT.ap[:, pts, hp, c * 64:(c + 1) * 64],
                                 start=(first and c == 0), stop=(last and c == 7),
                                 skip_group_check=True, tile_position=(0, 32 * hp))
                for hp in range(4):
                    lastm = e.matmul(PS[db][32 * hp:32 * hp + 32, :], lhsT=ones32, rhs=PT.ap[:, pts, hp, :],
                                     start=first, stop=last, tile_position=(0, 32 * hp))
                return lastm
            S.add("pe", fn, reads=VR.res(sl) + PT.res(pts) + ON32.res(), writes=[PSR[ob], PSR[db]])

        def finalize(od, ob, db, osb, qi):
            S.add("dve", lambda e: e.reciprocal(out=DI.ap[:, od, :], in_=PS[db][:, :]), reads=[PSR[db]], writes=DI.res(od))
            S.add("dve", lambda e: e.tensor_tensor(
                out=OS.ap[:, osb, :, qi * 64:(qi + 1) * 64],
                in0=PS[ob][:, :].rearrange("p (c q) -> p c q", c=8),
                in1=DI.ap[:, od, :].rearrange("p (c q) -> p c q", c=8), op=ALU.mult),
                reads=[PSR[ob]] + DI.res(od), writes=OS.res(osb))

        def store_o(i):
            t0 = (olo + 8 * i) * GW
            S.add("sp", lambda e: e.dma_start(out=otv[:, :, t0:t0 + 512], in_=OS.ap[:, i % 2]),
                  reads=OS.res(i % 2), writes=dres("ot", t0, t0 + 512), dma="so")

        load_seg(0)
        load_seg(1)
        load_seg(2)
        load_q(0)
        state = {"pt": 0, "od": 0}
        for i in range(ntile):
            flush()
            if i + 1 < ntile:
                load_seg(i + 3)
                load_q(i + 1)
            rho0 = olo + 8 * i
            qb = i % 2
            for qi in range(8):
                rho = rho0 + qi
                slots = cfg.slots(rho)
                ns = len(slots)
                od = state["od"]
                state["od"] = 1 - od
                ob, db = 4 + 2 * od, 5 + 2 * od
                for s_idx, ps_ in enumerate(slots):
                    sl = ((ps_ - rbase) // 8) % 4
                    ko = ((ps_ - rbase) % 8) * GW
                    pr = ((ps_ - rbase) % 8) // 2
                    di = ps_ - rho + 8
                    assert 0 <= di < 16
                    pts = state["pt"]
                    state["pt"] = (pts + 1) % 4
                    for H in range(2):
                        hps = (2 * H, 2 * H + 1)

                        def fs(e, sl=sl, ko=ko, qi=qi, hps=hps, qb=qb):
                            lastm = None
                            for c in range(8):
                                for hp in hps:
                                    lastm = e.matmul(PS[hp][:, c * 64:(c + 1) * 64],
                                                     lhsT=KR.ap[32 * hp:32 * hp + 32, c, sl, ko:ko + 128],
                                                     rhs=QB.ap[32 * hp:32 * hp + 32, qb, c, qi * 64:(qi + 1) * 64],
                                                     start=(c == 0), stop=(c == 7), skip_group_check=True,
                                                     tile_position=(32 * hp, 0))
                            return lastm
                        S.add("pe", fs, reads=kres(sl) + QB.res(qb), writes=[PSR[hps[0]], PSR[hps[1]]])

                        pres = PT.resb(pts * PT.csz + hps[0] * 1024, pts * PT.csz + hps[1] * 1024 + 1024)
                        S.add("act", lambda e, pts=pts, rho=rho, s_idx=s_idx, H=H: e.activation(
                            out=PTF.ap[:, pts, 1024 * H:1024 * H + 1024], in_=PSA[:, 1024 * H:1024 * H + 1024], func=AF.Exp,
                            bias=MT.ap[:, rho, s_idx:s_idx + 1], scale=1.0),
                            reads=[PSR[hps[0]], PSR[hps[1]]] + MT.res(), writes=pres)
                        if H == 0:
                            for hp in hps:
                                eng = "pool" if hp == 0 else "dve"
                                pr1 = PT.resb(pts * PT.csz + hp * 1024, pts * PT.csz + hp * 1024 + 1024)
                                S.add(eng, lambda e, pts=pts, hp=hp, di=di: e.tensor_tensor(
                                    out=PT.ap[:, pts, hp, :], in0=PT.ap[:, pts, hp, :], in1=TT.ap[:, di, hp, :], op=ALU.mult),
                                    reads=TT.res(di) + pr1, writes=pr1)
                        else:
                            S.add("dve", lambda e, pts=pts, di=di: e.tensor_tensor(
                                out=PTF.ap[:, pts, 1024:2048], in0=PTF.ap[:, pts, 1024:2048], in1=TTF.ap[:, di, 1024:2048], op=ALU.mult),
                                reads=TT.res(di) + pres, writes=pres)
                    flush(keep=1)
                    bundle = [lambda sl=sl, pr=pr, pts=pts, first=(s_idx == 0), last=(s_idx == ns - 1), ob=ob, db=db:
                              emit_pv(sl, pr, pts, first, last, ob, db)]
                    if s_idx == ns - 1:
                        bundle.append(lambda od=od, ob=ob, db=db, osb=i % 2, qi=qi: finalize(od, ob, db, osb, qi))
                        if qi == 7:
                            bundle.append(lambda i=i: store_o(i))
                    pend.append(bundle)
        flush()

    def mixb_pass(l, src, src_name, dst, dst_name):
        arena.reset()
        WC = arena.alloc([2, 8, 512], BF16)
        WA = arena.alloc([2, 8, 512], BF16)
        WO = arena.alloc([2, 8, 512], BF16)
        X32 = arena.alloc([8, 512], F32)
        ZB = arena.alloc([8, 520], F32)
        CBt = arena.alloc([8, 512], F32)
        OTt = arena.alloc([8, 512], BF16)
        GTt = arena.alloc([16, 512], F32)
        CV = arena.alloc([2, 8, 512], BF16)
        MX = arena.alloc([8, 512], BF16)
        Y32 = arena.alloc([8, 512], F32)
        CA = arena.alloc([2, 512], F32)
        T1 = arena.alloc([2, 512], F32)
        T2 = arena.alloc([2, 512], F32)
        SQ = arena.alloc([2, 512], BF16)
        RS = arena.alloc([1, 512], F32)
        FX = arena.alloc([2, 8], F32)
        for (W, wd) in ((WC, wcb), (WA, wab), (WO, wmo)):
            wv = wd.ap()[l].rearrange("(k p) n -> p k n", p=128)
            for g in range(2):
                S.add("pool", lambda e, W=W, wv=wv, g=g: e.dma_start(out=W.ap[:, g], in_=wv[:, :, g * 512:(g + 1) * 512]),
                      writes=W.res(g), dma="w")
        ztv, cbv, otv, gtv = fm(ZT), fm(CB), fm(OT), fm(GT)
        olo, ohi = cfg.out_rows(l)
        tiles = list(range(olo, ohi, 8))

        def cvres(par, c=None):
            if c is None:
                return CV.resb(par * CV.csz, par * CV.csz + CV.csz)
            return CV.resb(par * CV.csz + c * 1024, par * CV.csz + c * 1024 + 1024)

        def load_zc(t0):
            for c in range(8):
                S.add("sp", lambda e, c=c: e.dma_start(out=ZB.ap[:, c, 0:514], in_=ztv[:, c, t0 - 1:t0 + 513]),
                      reads=dres("zt", t0 - 1, t0 + 513, c), writes=ZB.res(c), dma="x")
                S.add("sp", lambda e, c=c: e.dma_start(out=CBt.ap[:, c, :], in_=cbv[:, c, t0:t0 + 512]),
                      reads=dres("cb", t0, t0 + 512, c), writes=CBt.res(c), dma="x")

        def load_ot(t0):
            for c in range(0, 8, 2):
                S.add("sp", lambda e, c=c: e.dma_start(out=OTt.ap[:, c:c + 2, :], in_=otv[:, c:c + 2, t0:t0 + 512]),
                      reads=dres("ot", t0, t0 + 512, [c, c + 1]), writes=OTt.res(c, 2), dma="x")

        def load_gt(t0, f):
            for gg in (f, 8 + f):
                S.add("sp", lambda e, gg=gg: e.dma_start(out=GTt.ap[:, gg, :], in_=gtv[:, gg, t0:t0 + 512]),
                      reads=dres("gt", t0, t0 + 512, gg), writes=GTt.res(gg), dma="x")

        def load_xc(t0, c):
            S.add("sp", lambda e: e.dma_start(out=X32.ap[:, c, :], in_=src[:, c, t0:t0 + 512]),
                  reads=dres(src_name, t0, t0 + 512, c), writes=X32.res(c), dma="x")

        def conv_chunk(t0, par, c):
            fixes = []
            for m, E in enumerate(cfg.edges):
                tE = E * GW
                if 0 <= tE - t0 < 512:
                    fixes.append((tE - t0, 0, tE - t0, m))
                if 0 <= tE - 1 - t0 < 512:
                    fixes.append((tE - 1 - t0, 2, tE - 1 - t0 + 2, m))
            a = c % 2
            S.add("act", lambda e: e.mul(out=CA.ap[:, a, :], in_=ZB.ap[:, c, 1:513], mul=CW.ap[:, l * 3 + 1, c:c + 1]),
                  reads=ZB.res(c) + CW.res(), writes=CA.res(a))
            S.add("dve", lambda e: e.scalar_tensor_tensor(out=CA.ap[:, a, :], in0=ZB.ap[:, c, 0:512], scalar=CW.ap[:, l * 3 + 0, c:c + 1],
                                                          in1=CA.ap[:, a, :], op0=ALU.mult, op1=ALU.add),
                  reads=ZB.res(c) + CW.res() + CA.res(a), writes=CA.res(a))
            S.add("dve", lambda e: e.scalar_tensor_tensor(out=CA.ap[:, a, :], in0=ZB.ap[:, c, 2:514], scalar=CW.ap[:, l * 3 + 2, c:c + 1],
                                                          in1=CA.ap[:, a, :], op0=ALU.mult, op1=ALU.add),
                  reads=ZB.res(c) + CW.res() + CA.res(a), writes=CA.res(a))
            for (j, wj, zc, m) in fixes:
                S.add("dve", lambda e, wj=wj, zc=zc: e.tensor_scalar(out=FX.ap[:, a, 0:1], in0=ZB.ap[:, c, zc:zc + 1],
                                                                     scalar1=CW.ap[:, l * 3 + wj, c:c + 1], scalar2=None, op0=ALU.mult),
                      reads=ZB.res(c) + CW.res(), writes=FX.res(a))
                S.add("dve", lambda e, j=j, m=m: e.scalar_tensor_tensor(out=CA.ap[:, a, j:j + 1], in0=FX.ap[:, a, 0:1], scalar=EF.ap[:, 0, m:m + 1],
                                                                        in1=CA.ap[:, a, j:j + 1], op0=ALU.mult, op1=ALU.add),
                      reads=FX.res(a) + EF.res() + CA.res(a), writes=CA.res(a))
            S.add("dve", lambda e: e.tensor_tensor(out=CV.ap[:, par, c, :], in0=CA.ap[:, a, :], in1=CBt.ap[:, c, :], op=ALU.mult),
                  reads=CA.res(a) + CBt.res(c), writes=cvres(par, c))

        t00 = tiles[0] * GW
        load_zc(t00)
        load_ot(t00)
        for f in range(8):
            load_gt(t00, f)
        for c in range(8):
            load_xc(t00, c)
        for c in range(8):
            conv_chunk(t00, 0, c)
        deferred = None
        for ti, rho0 in enumerate(tiles):
            t0 = rho0 * GW
            par = ti % 2
            nxt = tiles[ti + 1] * GW if ti + 1 < len(tiles) else None
            if nxt is not None:
                load_zc(nxt)
            for f in range(8):
                g, i = f // 4, f % 4
                b1, b2 = 2 * (f % 2), 2 * (f % 2) + 1
                mm_group(b1, [(WC.ap[:, g, k, i * 128:(i + 1) * 128], CV.ap[:, par, k, :], PS[b1][:, :]) for k in range(8)], WC.res(g) + cvres(par))
                mm_group(b2, [(WA.ap[:, g, k, i * 128:(i + 1) * 128], OTt.ap[:, k, :], PS[b2][:, :]) for k in range(8)], WA.res(g) + OTt.res())
                a = f % 2
                S.add("dve", lambda e, f=f, a=a, b1=b1: e.tensor_tensor(out=T1.ap[:, a, :], in0=GTt.ap[:, f, :], in1=PS[b1][:, :], op=ALU.mult),
                      reads=GTt.res(f) + [PSR[b1]], writes=T1.res(a))
                S.add("dve", lambda e, f=f, a=a, b2=b2: e.tensor_tensor(out=T2.ap[:, a, :], in0=GTt.ap[:, 8 + f, :], in1=PS[b2][:, :], op=ALU.mult),
                      reads=GTt.res(8 + f) + [PSR[b2]], writes=T2.res(a))
                S.add("pool", lambda e, f=f, a=a: e.tensor_tensor(out=MX.ap[:, f, :], in0=T1.ap[:, a, :], in1=T2.ap[:, a, :], op=ALU.add),
                      reads=T1.res(a) + T2.res(a), writes=MX.res(f))
                if deferred is not None:
                    deferred(f)
                if nxt is not None:
                    load_gt(nxt, f)
                    conv_chunk(nxt, 1 - par, f)
            if nxt is not None:
                load_ot(nxt)

            def terms_fn(f, out):
                g, i = f // 4, f % 4
                return [(WO.ap[:, g, k, i * 128:(i + 1) * 128], MX.ap[:, k, :], out) for k in range(8)]

            def reads_fn(f):
                return WO.res(f // 4) + MX.res()

            def st(f, t0=t0, nxt=nxt):
                S.add("pool", lambda e: e.dma_start(out=dst[:, f, t0:t0 + 512], in_=X32.ap[:, f, :]),
                      reads=X32.res(f), writes=dres(dst_name, t0, t0 + 512, f), dma="s")
                if nxt is not None:
                    load_xc(nxt, f)
            post_part1(Y32, SQ, RS, 0, terms_fn, reads_fn)

            def deferred(f, st=st):
                post_part2(X32, Y32, RS, 0, l, 3, False, f, after=st)
        for f in range(8):
            deferred(f)

    import os
    dbg = os.environ.get("K_PASSES", "")
    if dbg:
        names = dbg.split(",")
        l = 0
        if "ffn1" in names:
            ffn_pass(l, w1i, w1o, 0, 1, fm(xin), "xin", fm(XR[0]), "xr0", cfg.in_rows(l), 0)
        if "mixa" in names:
            mixa_pass(l, fm(xin), "xin", cfg.in_rows(l))
        if "att" in names:
            att_pass(l)
        if "mixb" in names:
            mixb_pass(l, fm(xin), "xin", fm(XR[1]), "xr1")
        arena.reset()
        TB = arena.alloc([8, 512], F32)
        dsrc = {"ffn1": XR[0], "mixa": ZT, "att": ZT, "mixb": XR[1]}[names[-1]]
        dn = {"ffn1": "xr0", "mixa": "zt", "att": "zt", "mixb": "xr1"}[names[-1]]
        for rho0 in range(cfg.H, cfg.H + cfg.RB, 8):
            t0 = rho0 * GW
            S.add("sp", lambda e, t0=t0: e.dma_start(out=TB.ap, in_=fm(dsrc)[:, :, t0:t0 + 512]), reads=dres(dn, t0, t0 + 512), writes=TB.res(), dma="x")
            S.add("pool", lambda e, t0=t0: e.dma_start(out=fm(yout)[:, :, t0 - cfg.H * GW:t0 - cfg.H * GW + 512], in_=TB.ap), reads=TB.res(),
                  writes=dres("yout", t0 - cfg.H * GW, t0 - cfg.H * GW + 512), dma="s")
        S.add("sp", lambda e: None, reads=list(dres_tab["yout"].values()))
        S.emit()
        return nc
    for l in range(NL):
        if l == 0:
            src, sname = fm(xin), "xin"
        else:
            src, sname = fm(XR[1]), "xr1"
        ffn_pass(l, w1i, w1o, 0, 1, src, sname, fm(XR[0]), "xr0", cfg.in_rows(l), 0)
        mixa_pass(l, fm(XR[0]), "xr0", cfg.in_rows(l))
        att_pass(l)
        mixb_pass(l, fm(XR[0]), "xr0", fm(XR[1]), "xr1")
        if l == NL - 1:
            ffn_pass(l, w2i, w2o, 4, 5, fm(XR[1]), "xr1", fm(yout), "yout", cfg.out_rows(l), cfg.H * GW)
        else:
            ffn_pass(l, w2i, w2o, 4, 5, fm(XR[1]), "xr1", fm(XR[1]), "xr1", cfg.out_rows(l), 0)
    S.add("sp", lambda e: None, reads=list(dres_tab["yout"].values()))
    S.emit()
    return nc


def host_tables(cfg, c):
    RT, RB, H = cfg.RT, cfg.RB, cfg.H
    starts = np.concatenate([[0], np.cumsum(cfg.seqs)])
    total = int(starts[-1])

    def true_window(g):
        if g < 0 or g >= total:
            return (g - 4, g + 3)
        si = int(np.searchsorted(starts, g, side="right") - 1)
        gs, L = int(starts[si]), cfg.seqs[si]
        rs = int(np.clip(g - gs - 4, 0, L - 8))
        return (gs + rs, gs + rs + 7)

    mt = np.zeros((128, RT, 6), np.float32)
    for rho in range(RT):
        g = RB * c + rho - H
        lo, hi = true_window(g)
        for s, ps in enumerate(cfg.slots(rho)):
            for kr in range(2):
                gk = RB * c + ps + kr - H
                if not (lo <= gk <= hi):
                    mt[kr * 64:(kr + 1) * 64, rho, s] = NEG
    ef = np.zeros((128, 8), np.float32)
    for m, E in enumerate(cfg.edges):
        g = RB * c + E - H
        if g <= 0 or g >= total or g in set(int(x) for x in starts):
            ef[:, m] = -1.0
    return mt.reshape(128, RT * 6), ef


def bias_table(rpb_l):
    kc = np.arange(64)[:, None]
    qc = np.arange(64)[None, :]
    cs = np.clip(qc - 8, 0, 48)
    ok = (kc >= cs) & (kc < cs + 16)
    dci = np.clip(kc - qc, -15, 15) + 15
    B = np.full((NH, 17, 64, 64), NEG, np.float32)
    G = rpb_l[:, :, dci]
    G = np.where(ok[None, None], G, np.float32(NEG))
    B[:, 1:16] = G
    T = np.empty((2, 64, 16, 4, 8, 64), np.float32)
    Bh = B.reshape(8, 4, 17, 64, 64)
    for kr in range(2):
        T[kr] = np.transpose(Bh[:, :, kr:kr + 16], (3, 2, 1, 0, 4))
    return T.reshape(128, 16 * 2048)


def prepare(cfg, inp):
    NL = cfg.NL
    xs = [np.asarray(inp["x_prompt"], np.float32), np.asarray(inp["x_sample"], np.float32)]
    glob = np.concatenate([x.reshape(-1, D) for x in xs], axis=0)
    total_rows = sum(cfg.seqs)
    assert glob.shape[0] == total_rows * GW
    pad = np.zeros((cfg.H * GW, D), np.float32)
    gp = np.concatenate([pad, glob, pad], axis=0)

    def pvec(a, inner):
        a = np.asarray(a, np.float32)
        sh = a.shape[:-1]
        a = a.reshape(*sh, inner, 128)
        a = np.moveaxis(a, -1, 0)
        return np.ascontiguousarray(a.reshape(128, -1))

    gains = np.stack([np.asarray(inp[k], np.float32)[:NL] for k in
                      ("g_ffn1_pre", "g_ffn1_post", "g_mix_pre", "g_mix_post", "g_ffn2_pre", "g_ffn2_post")], axis=1)
    common = {
        "gvec": pvec(gains, 8),
        "bgate": pvec(np.asarray(inp["b_mix_gate"])[:NL], 16),
        "convw": pvec(np.asarray(inp["conv_w"])[:NL], 8),
        "ttab": np.stack([bias_table(np.asarray(inp["na_rpb"], np.float32)[l]) for l in range(NL)], axis=0),
        "ident": np.eye(128, dtype=np.float32),
        "onesm": np.full((128, 128), 1.0 / D, np.float32),
    }
    for k in ("w_ffn1_in", "w_ffn1_out", "w_mix_in", "w_conv_branch", "w_att_branch", "w_mix_out", "w_ffn2_in", "w_ffn2_out"):
        common[k] = np.ascontiguousarray(np.asarray(inp[k], np.float32)[:NL])
    maps = []
    for c in range(cfg.ncores):
        r0 = cfg.RB * c
        xin = np.ascontiguousarray(gp[r0 * GW:(r0 + cfg.RT) * GW].T)
        mt, ef = host_tables(cfg, c)
        m = dict(common)
        m["xin"] = xin
        m["mtab"] = mt
        m["eflag"] = ef
        maps.append(m)
    return maps


def run(cfg, inp):
    nc = build_program(cfg)
    maps = prepare(cfg, inp)
    import os
    ncr = int(os.environ.get("K_NCORES", cfg.ncores))
    if os.environ.get("K_TRACE"):
        res = run_bass_kernel_spmd(nc, maps[:ncr], core_ids=list(range(ncr)), trace=True)
        print("EXEC_TIME_NS", res.exec_time_ns)
    else:
        res = run_bass_kernel_spmd(nc, maps[:ncr], core_ids=list(range(ncr)))
    ys = [np.asarray(r["yout"], np.float32).T for r in res.results]
    ys = ys + [ys[0]] * (cfg.ncores - ncr)
    glob = np.concatenate(ys, axis=0)
    np_ = int(np.asarray(inp["x_prompt"]).shape[0] * np.asarray(inp["x_prompt"]).shape[1])
    yp = glob[:np_].reshape(np.asarray(inp["x_prompt"]).shape)
    ysm = glob[np_:].reshape(np.asarray(inp["x_sample"]).shape)
    return (np.ascontiguousarray(yp), np.ascontiguousarray(ysm))


def kernel(**inputs):
    cfg = Cfg()
    return run(cfg, inputs)
```
